# Optimizing a Trainium2 kernel written in Bass

```python
import jax, jax.numpy as jnp
from jax import lax
import numpy as np

D_MODEL = 1024
BATCH = 4
SEQ = 8192
DEPTH = 2

GRID_W = 64
CTX_LEN = 256
HEAD_DIM = 64
MIX_WIDTH = D_MODEL
A_HEADS = 6
A_KV = 2
A_WINDOW = 128
A_BLOCK = 128
B_HEADS = 4
B_KV = 2
B_BLOCK = 128
C_HEADS = 6
NA_KH = 8
NA_KW = 16
D_FF = 2816
ROPE_THETA = 10000.0
EPS = 1e-6
N_MOD = 9
NEG = -1e30
KV_WIDTHS = (A_KV, A_KV, B_KV, B_KV, C_HEADS, C_HEADS)
W_IN_COLS = MIX_WIDTH + sum(KV_WIDTHS) * HEAD_DIM

kernel_name = "hybrid_dit_parallel_heads_ctx_prefix"


def _rms_norm(x, g):
    xf = x.astype(jnp.float32)
    y = xf * lax.rsqrt(jnp.mean(xf * xf, axis=-1, keepdims=True) + EPS)
    return (y * g.astype(jnp.float32)).astype(x.dtype)


def _modulate(u, shift, scale):
    return u * (1.0 + scale) + shift


def _swiglu(u, w_gate, w_up, w_down):
    return (jax.nn.silu(u @ w_gate) * (u @ w_up)) @ w_down


def _axial_rope(S):
    t = jnp.arange(S)
    row = (t // GRID_W).astype(jnp.float32)
    col = (t % GRID_W).astype(jnp.float32)
    n_freq = HEAD_DIM // 4
    inv = ROPE_THETA ** (-jnp.arange(n_freq, dtype=jnp.float32) / n_freq)
    ang = jnp.concatenate([row[:, None] * inv, col[:, None] * inv], axis=-1)
    return jnp.cos(ang), jnp.sin(ang)


def _rope(x, cos, sin):
    half = HEAD_DIM // 2
    x1, x2 = x[..., :half], x[..., half:]
    cs, sn = cos[None, :, None, :], sin[None, :, None, :]
    return jnp.concatenate([x1 * cs - x2 * sn, x2 * cs + x1 * sn], axis=-1).astype(x.dtype)


def _multi_softmax(logits):
    sizes = [int(l.shape[-1]) for l in logits]
    p = jax.nn.softmax(jnp.concatenate(logits, axis=-1), axis=-1)
    return jnp.split(p, [int(s) for s in np.cumsum(sizes)[:-1]], axis=-1)


def _heads(t, n):
    return t.reshape(*t.shape[:-1], n, HEAD_DIM)


def _split_q(p):
    a = A_HEADS * HEAD_DIM
    b = B_HEADS * HEAD_DIM
    return (_heads(p[..., :a], A_HEADS), _heads(p[..., a:a + b], B_HEADS),
            _heads(p[..., a + b:MIX_WIDTH], C_HEADS))


def _split_kv(p):
    offs = [0]
    for w in KV_WIDTHS:
        offs.append(offs[-1] + w * HEAD_DIM)
    return [_heads(p[..., offs[i]:offs[i + 1]], KV_WIDTHS[i]) for i in range(len(KV_WIDTHS))]


def _ctx_attn(qc, kc, vc, sink=None):
    Bn, L, H, d = qc.shape
    KV = kc.shape[2]
    G = H // KV
    qg = qc.reshape(Bn, L, KV, G, d)
    s = jnp.einsum('bqkgd,blkd->bkgql', qg, kc).astype(jnp.float32) * (d ** -0.5)
    if sink is None:
        p = jax.nn.softmax(s, axis=-1)
    else:
        sink_col = jnp.broadcast_to(sink.astype(jnp.float32).reshape(1, KV, G, 1, 1), (Bn, KV, G, L, 1))
        p, _ = _multi_softmax([s, sink_col])
    out = jnp.einsum('bkgql,blkd->bqkgd', p.astype(vc.dtype), vc)
    return out.reshape(Bn, L, H * d)


def _window_attn(q, k, v, kc, vc, sink):
    Bn, S, H, d = q.shape
    KV = k.shape[2]
    G = H // KV
    nb = S // A_BLOCK
    band = 3 * A_BLOCK
    scale = d ** -0.5
    pad = ((0, 0), (A_BLOCK, A_BLOCK), (0, 0), (0, 0))
    kp, vp = jnp.pad(k, pad), jnp.pad(v, pad)
    qb = jnp.moveaxis(q.reshape(Bn, nb, A_BLOCK, KV, G, d), 1, 0)
    rel = jnp.arange(band)[None, :] - A_BLOCK - jnp.arange(A_BLOCK)[:, None]
    sink_col = jnp.broadcast_to(sink.astype(jnp.float32).reshape(1, KV, G, 1, 1), (Bn, KV, G, A_BLOCK, 1))

    def block(args):
        n, q_n = args
        start = n * A_BLOCK
        k_n = lax.dynamic_slice_in_dim(kp, start, band, axis=1)
        v_n = lax.dynamic_slice_in_dim(vp, start, band, axis=1)
        kpos = start - A_BLOCK + jnp.arange(band)
        mask = (jnp.abs(rel) <= A_WINDOW) & ((kpos >= 0) & (kpos < S))[None, :]
        s_w = jnp.einsum('bqkgd,bskd->bkgqs', q_n, k_n).astype(jnp.float32) * scale
        s_w = jnp.where(mask, s_w, NEG)
        s_c = jnp.einsum('bqkgd,blkd->bkgql', q_n, kc).astype(jnp.float32) * scale
        p_w, p_c, _ = _multi_softmax([s_w, s_c, sink_col])
        return (jnp.einsum('bkgqs,bskd->bqkgd', p_w.astype(v.dtype), v_n)
                + jnp.einsum('bkgql,blkd->bqkgd', p_c.astype(vc.dtype), vc))

    out = lax.map(block, (jnp.arange(nb), qb))
    return jnp.moveaxis(out, 0, 1).reshape(Bn, S, H * d)


def _global_attn(q, k, v, kc, vc):
    Bn, S, H, d = q.shape
    KV = k.shape[2]
    G = H // KV
    nb = S // B_BLOCK
    scale = d ** -0.5
    qb = jnp.moveaxis(q.reshape(Bn, nb, B_BLOCK, KV, G, d), 1, 0)

    def block(q_n):
        s_l = jnp.einsum('bqkgd,bskd->bkgqs', q_n, k).astype(jnp.float32) * scale
        s_c = jnp.einsum('bqkgd,blkd->bkgql', q_n, kc).astype(jnp.float32) * scale
        p_l, p_c = _multi_softmax([s_l, s_c])
        return (jnp.einsum('bkgqs,bskd->bqkgd', p_l.astype(v.dtype), v)
                + jnp.einsum('bkgql,blkd->bqkgd', p_c.astype(vc.dtype), vc))

    out = lax.map(block, qb)
    return jnp.moveaxis(out, 0, 1).reshape(Bn, S, H * d)


def _neighbourhood_attn(q, k, v, kc, vc, rpb):
    Bn, S, H, d = q.shape
    rows = S // GRID_W
    kh = min(NA_KH, rows)
    kw = NA_KW
    n_keys = kh * kw
    scale = d ** -0.5
    qg = jnp.moveaxis(q.reshape(Bn, rows, GRID_W, H, d), 1, 0)
    kg = k.reshape(Bn, rows, GRID_W, H, d)
    vg = v.reshape(Bn, rows, GRID_W, H, d)
    cols = np.arange(GRID_W)
    col_idx = np.clip(cols - kw // 2, 0, GRID_W - kw)[:, None] + np.arange(kw)[None, :]
    dc = col_idx - cols[:, None] + NA_KW - 1
    rpb_col = rpb[:, :, dc]

    def gather(t, rs):
        t_rows = lax.dynamic_slice_in_dim(t, rs, kh, axis=1)
        t_nb = t_rows[:, :, col_idx]
        return jnp.transpose(t_nb, (0, 2, 1, 3, 4, 5)).reshape(Bn, GRID_W, n_keys, H, d)

    def block(args):
        r, q_r = args
        rs = jnp.clip(r - kh // 2, 0, rows - kh)
        dr = rs + jnp.arange(kh) - r + NA_KH - 1
        bias = jnp.transpose(jnp.take(rpb_col, dr, axis=1), (0, 2, 1, 3)).reshape(H, GRID_W, n_keys)
        k_nb, v_nb = gather(kg, rs), gather(vg, rs)
        s_n = jnp.einsum('bwhd,bwnhd->bhwn', q_r, k_nb).astype(jnp.float32) * scale + bias.astype(jnp.float32)
        s_c = jnp.einsum('bwhd,blhd->bhwl', q_r, kc).astype(jnp.float32) * scale
        p_n, p_c = _multi_softmax([s_n, s_c])
        return (jnp.einsum('bhwn,bwnhd->bwhd', p_n.astype(v.dtype), v_nb)
                + jnp.einsum('bhwl,blhd->bwhd', p_c.astype(vc.dtype), vc))

    out = lax.map(block, (jnp.arange(rows), qg))
    return jnp.moveaxis(out, 0, 1).reshape(Bn, S, H * d)


def setup_inputs(seed: int = 0) -> dict:
    key = jax.random.key(seed)
    ks = jax.random.split(key, 24)
    f32 = jnp.float32
    D, F = D_MODEL, D_FF

    def nrm(k, shape, s):
        return jax.random.normal(k, shape, f32) * s

    def gain(k, shape):
        return 1.0 + 0.02 * jax.random.normal(k, shape, f32)

    return {
        "x": nrm(ks[0], (BATCH, SEQ, D), 1.0),
        "c": nrm(ks[1], (BATCH, D), 1.0),
        "ctx": nrm(ks[2], (BATCH, CTX_LEN, D), 1.0),
        "c_ctx": nrm(ks[3], (D,), 1.0),
        "w_ada": nrm(ks[4], (DEPTH, D, N_MOD * D), 0.5 * D ** -0.5),
        "b_ada": nrm(ks[5], (DEPTH, N_MOD * D), 0.02),
        "norm_ffn1": gain(ks[6], (DEPTH, D)),
        "w_ffn1_gate": nrm(ks[7], (DEPTH, D, F), D ** -0.5),
        "w_ffn1_up": nrm(ks[8], (DEPTH, D, F), D ** -0.5),
        "w_ffn1_down": nrm(ks[9], (DEPTH, F, D), F ** -0.5),
        "norm_mix": gain(ks[10], (DEPTH, D)),
        "w_in": nrm(ks[11], (DEPTH, D, W_IN_COLS), D ** -0.5),
        "q_norm_glob": gain(ks[12], (DEPTH, HEAD_DIM)),
        "k_norm_glob": gain(ks[13], (DEPTH, HEAD_DIM)),
        "sink_win": nrm(ks[14], (DEPTH, A_HEADS), 0.5),
        "rpb_nbr": nrm(ks[15], (DEPTH, C_HEADS, 2 * NA_KH - 1, 2 * NA_KW - 1), 0.1),
        "w_out": nrm(ks[16], (DEPTH, MIX_WIDTH, D), MIX_WIDTH ** -0.5),
        "norm_ffn2": gain(ks[17], (DEPTH, D)),
        "w_ffn2_gate": nrm(ks[18], (DEPTH, D, F), D ** -0.5),
        "w_ffn2_up": nrm(ks[19], (DEPTH, D, F), D ** -0.5),
        "w_ffn2_down": nrm(ks[20], (DEPTH, F, D), F ** -0.5),
        "norm_final": gain(ks[21], (D,)),
    }


def reference(x, c, ctx, c_ctx, w_ada, b_ada, norm_ffn1, w_ffn1_gate, w_ffn1_up, w_ffn1_down,
              norm_mix, w_in, q_norm_glob, k_norm_glob, sink_win, rpb_nbr, w_out,
              norm_ffn2, w_ffn2_gate, w_ffn2_up, w_ffn2_down, norm_final):
    S = x.shape[1]
    cos, sin = _axial_rope(S)
    h, hc = x, ctx
    for l in range(DEPTH):
        last = l == DEPTH - 1
        mod_x = (jax.nn.silu(c) @ w_ada[l] + b_ada[l])[:, None, :]
        mod_c = jax.nn.silu(c_ctx) @ w_ada[l] + b_ada[l]
        sh1, sc1, g1, shm, scm, gm, sh2, sc2, g2 = jnp.split(mod_x, N_MOD, axis=-1)
        csh1, csc1, cg1, cshm, cscm, cgm, csh2, csc2, cg2 = jnp.split(mod_c, N_MOD, axis=-1)

        h = h + 0.5 * g1 * _swiglu(_modulate(_rms_norm(h, norm_ffn1[l]), sh1, sc1),
                                   w_ffn1_gate[l], w_ffn1_up[l], w_ffn1_down[l])
        hc = hc + 0.5 * cg1 * _swiglu(_modulate(_rms_norm(hc, norm_ffn1[l]), csh1, csc1),
                                      w_ffn1_gate[l], w_ffn1_up[l], w_ffn1_down[l])

        u = _modulate(_rms_norm(h, norm_mix[l]), shm, scm)
        uc = _modulate(_rms_norm(hc, norm_mix[l]), cshm, cscm)
        p = u @ w_in[l]
        q_w, q_g, q_n = _split_q(p[..., :MIX_WIDTH])
        k_w, v_w, k_g, v_g, k_n, v_n = _split_kv(p[..., MIX_WIDTH:])
        if last:
            kv_c = uc @ w_in[l][:, MIX_WIDTH:]
        else:
            pc = uc @ w_in[l]
            cq_w, cq_g, cq_n = _split_q(pc[..., :MIX_WIDTH])
            kv_c = pc[..., MIX_WIDTH:]
        ck_w, cv_w, ck_g, cv_g, ck_n, cv_n = _split_kv(kv_c)

        q_g = _rms_norm(q_g, q_norm_glob[l])
        k_g = _rms_norm(k_g, k_norm_glob[l])
        ck_g = _rms_norm(ck_g, k_norm_glob[l])
        q_w, k_w = _rope(q_w, cos, sin), _rope(k_w, cos, sin)
        q_g, k_g = _rope(q_g, cos, sin), _rope(k_g, cos, sin)

        y_w = _window_attn(q_w, k_w, v_w, ck_w, cv_w, sink_win[l])
        y_g = _global_attn(q_g, k_g, v_g, ck_g, cv_g)
        y_n = _neighbourhood_attn(q_n, k_n, v_n, ck_n, cv_n, rpb_nbr[l])
        h = h + gm * (jnp.concatenate([y_w, y_g, y_n], axis=-1) @ w_out[l])

        if not last:
            cq_g = _rms_norm(cq_g, q_norm_glob[l])
            yc = jnp.concatenate([_ctx_attn(cq_w, ck_w, cv_w, sink_win[l]),
                                  _ctx_attn(cq_g, ck_g, cv_g),
                                  _ctx_attn(cq_n, ck_n, cv_n)], axis=-1)
            hc = hc + cgm * (yc @ w_out[l])

        h = h + 0.5 * g2 * _swiglu(_modulate(_rms_norm(h, norm_ffn2[l]), sh2, sc2),
                                   w_ffn2_gate[l], w_ffn2_up[l], w_ffn2_down[l])
        if not last:
            hc = hc + 0.5 * cg2 * _swiglu(_modulate(_rms_norm(hc, norm_ffn2[l]), csh2, csc2),
                                          w_ffn2_gate[l], w_ffn2_up[l], w_ffn2_down[l])
    return _rms_norm(h, norm_final)
```

```python
import numpy as np
import ml_dtypes
from contextlib import ExitStack
import concourse.bass as bass
import concourse.mybir as mybir
from concourse.bass_utils import run_bass_kernel_spmd

F32 = mybir.dt.float32
BF16 = mybir.dt.bfloat16
AF = mybir.ActivationFunctionType
ALU = mybir.AluOpType
NPBF = ml_dtypes.bfloat16

D = 1024
FF = 2816
NF = 22
TC = 256
EPS = 1e-6
GRID_W = 64
NT = 1024


class Buf:
    __slots__ = ("name", "w", "r", "dram")

    def __init__(self, name, dram=False):
        self.name = name
        self.w = {}
        self.r = {}
        self.dram = dram


class KB:
    def __init__(self, nc, es):
        self.nc = nc
        self.es = es
        self.eng = {"pe": nc.tensor, "act": nc.scalar, "dve": nc.vector,
                    "pool": nc.gpsimd, "sp": nc.sync}
        self.sem = {}
        self.cnt = {}
        self.seen = {e: {} for e in self.eng}
        for e in self.eng:
            self.sem[e] = es.enter_context(nc.semaphore("s_" + e))
            self.cnt[e] = 0
        self.dsem = {}
        self.dcnt = {}
        self.ninstr = 0
        self.pending = None

    def _wait(self, e, key, val):
        if key == e and e in ("pe", "sp"):
            return
        if self.seen[e].get(key, 0) >= val:
            return
        sem = self.sem[key] if key in self.sem else self.dsem[key]
        if self.pending is not None:
            self.pending = [p for p in self.pending if p[0] != key]
            self.pending.append((key, sem, val))
        else:
            self.eng[e].wait_ge(sem, val)
        self.seen[e][key] = val

    def _deps(self, e, reads, writes):
        for b in reads:
            for k, v in b.w.items():
                self._wait(e, k, v)
        for b in writes:
            if not b.dram:
                for k, v in b.w.items():
                    self._wait(e, k, v)
            for k, v in b.r.items():
                self._wait(e, k, v)

    def _mark(self, key, val, reads, writes):
        for b in reads:
            if b.r.get(key, 0) < val:
                b.r[key] = val
        for b in writes:
            if b.dram:
                b.w[key] = val
            else:
                b.w = {key: val}
                b.r = {}

    def op(self, e, fn, reads=(), writes=(), inc=True):
        self.pending = []
        self._deps(e, reads, writes)
        pend, self.pending = self.pending, None
        for (key, sem, val) in pend[:-1]:
            self.eng[e].wait_ge(sem, val)
        ins = fn(self.eng[e])
        if pend:
            ins._wait_ge(pend[-1][1], pend[-1][2])
        self.ninstr += 1
        if inc:
            self.cnt[e] += 1
            ins.then_inc(self.sem[e], 1)
            val = self.cnt[e]
        else:
            val = self.cnt[e] + 1
        self._mark(e, val, reads, writes)

    def dma(self, q, out, in_, reads, writes, key=None):
        self._deps(q, reads, writes)
        if key is None:
            sbw = [b for b in writes if not b.dram]
            sbr = [b for b in reads if not b.dram]
            key = ("ld" + sbw[0].name) if sbw else ("st" + sbr[0].name)
        if key not in self.dsem:
            self.dsem[key] = self.es.enter_context(self.nc.semaphore("d_" + key))
            self.dcnt[key] = 0
        self.dcnt[key] += 16
        self.eng[q].dma_start(out=out, in_=in_).then_inc(self.dsem[key], 16)
        self.ninstr += 1
        self._mark(key, self.dcnt[key], reads, writes)

    def barrier(self):
        for e in self.eng:
            for o in self.eng:
                if o != e and self.cnt[o] > 0:
                    self._wait(e, o, self.cnt[o])
            for key, val in self.dcnt.items():
                self._wait(e, key, val)

    def finish(self):
        for key, val in self.dcnt.items():
            self._wait("sp", key, val)


class WStream:
    def __init__(self, K, rings, tag=""):
        self.K = K
        self.tag = tag
        self.rings = rings
        self.plan = {k: [] for k in rings}
        self.emitted = {k: 0 for k in rings}
        self.released = {k: 0 for k in rings}
        self.taken = {k: 0 for k in rings}

    def add(self, kind, loads):
        self.plan[kind].append(loads)

    def pump(self, kind):
        ring = self.rings[kind]
        while (self.emitted[kind] < self.released[kind] + len(ring)
               and self.emitted[kind] < len(self.plan[kind])):
            i = self.emitted[kind]
            t, b = ring[i % len(ring)]
            for dst_fn, src in self.plan[kind][i]:
                self.K.dma("pool", dst_fn(t), src, [], [b], "w%s%s%d" % (self.tag, kind, i % len(ring)))
            self.emitted[kind] += 1

    def start(self):
        for k in self.rings:
            self.pump(k)

    def take(self, kind):
        i = self.taken[kind]
        assert i < self.emitted[kind], (kind, i)
        self.taken[kind] += 1
        ring = self.rings[kind]
        return ring[i % len(ring)]

    def release(self, kind):
        self.released[kind] += 1
        self.pump(kind)


def _blocks(n, w=512):
    return [(o, min(w, n - o)) for o in range(0, n, w)]


QK_ITEMS = [
    ("rope", "q", 0), ("rope", "q", 128), ("rope", "q", 256),
    ("nrope", "q", 384), ("nrope", "q", 512),
    ("rope", "k", 0),
    ("nrope", "k", 128),
    ("plain", "q", 640), ("plain", "q", 768), ("plain", "q", 896),
    ("plain", "k", 256), ("plain", "k", 384), ("plain", "k", 512),
]


class Prog:
    def __init__(self, mode, TL):
        self.mode = mode
        self.TL = TL
        self.TT = TL + TC
        self.nc = bass.Bass("TRN2", target_bir_lowering=False)
        self.dram = {}

    def din(self, name, shape, dt=F32):
        self.dram[name] = self.nc.dram_tensor(name, list(shape), dt, kind="ExternalInput").ap()
        return self.dram[name]

    def dout(self, name, shape, dt=F32):
        self.dram[name] = self.nc.dram_tensor(name, list(shape), dt, kind="ExternalOutput").ap()
        return self.dram[name]

    def dtmp(self, name, shape, dt=F32):
        self.dram[name] = self.nc.dram_tensor(name, list(shape), dt, kind="Internal").ap()
        return self.dram[name]


PHASES = {
    "T0": [("T", None, 0, False)],
    "A0T1": [("A", 0), ("T", 0, 1, False)],
    "A1T2": [("A", 1), ("T", 1, None, True)],
    "FUSED": [("T", None, 0, False), ("X", 0), ("A", 0), ("T", 0, 1, False), ("X", 1), ("A", 1),
              ("T", 1, None, True)],
    "L1": [("T", None, 0, False), ("X", 0), ("A", 0), ("T", 0, 1, False)],
}


def xb_layout(TL):
    pieces = [("kg", [128, TL]), ("vg", [TL, 128]),
              ("kw_first", [128, 128]), ("kw_last", [128, 128]),
              ("vw_first", [128, 128]), ("vw_last", [128, 128]),
              ("kn_first", [384, 256]), ("kn_last", [384, 256]),
              ("vn_first", [256, 384]), ("vn_last", [256, 384])]
    lay = {}
    r = 0
    for nm, (a, b) in pieces:
        n = a * b // 1024
        lay[nm] = (r, (a, b))
        r += n
    return lay, r


def xb_view(buf, r0, shape):
    a, b = shape
    n = a * b // 1024
    rows = buf[r0:r0 + n, :]
    if b >= 1024:
        return rows.rearrange("(a f) c -> a (f c)", f=b // 1024)
    if 1024 % b == 0:
        return rows.rearrange("r (e d) -> (r e) d", d=b)
    raise ValueError(shape)


def xb_pieces(TL):
    lay, NR = xb_layout(TL)
    a = lay["vg"][0]
    b = lay["kw_first"][0]
    return [(0, a), (a, b - a), (b, NR - b)]


def gb_row(TL, rank, r0):
    for (off, n) in xb_pieces(TL):
        if off <= r0 < off + n:
            return 2 * off + rank * n + (r0 - off)
    raise ValueError(r0)


def gb_from_xb(xb0, xb1, TL):
    parts = []
    for (off, n) in xb_pieces(TL):
        parts += [xb0[off:off + n], xb1[off:off + n]]
    return np.ascontiguousarray(np.concatenate(parts, axis=0))


def build(mode, TL=4096, groups=None):
    P = Prog(mode, TL)
    nc = P.nc
    TT = P.TT
    NQT = TL // 512
    fused = mode == "FUSED"
    phases = PHASES[mode]
    t_phases = [p for p in phases if p[0] == "T"]
    a_layers = [p[1] for p in phases if p[0] == "A"]
    mod_layers = sorted({l for p in t_phases for l in (p[1], p[2]) if l is not None})
    proj_layers = [p[2] for p in t_phases if p[2] is not None]
    prev_layers = [p[1] for p in t_phases if p[1] is not None]
    has_final = any(p[3] for p in t_phases)
    XL, NR = xb_layout(TL)
    es = ExitStack()
    with es:
        K = KB(nc, es)

        def sb(name, shape, dt):
            return es.enter_context(nc.sbuf_tensor(name, list(shape), dt, align_bytes=256))

        ps = [es.enter_context(nc.psum_tensor("ps%d" % i, [128, 512], F32)) for i in range(8)]
        Bps = [Buf("ps%d" % i) for i in range(8)]

        ones_bf = sb("ones_bf", [128, 128], BF16)
        B_ones = Buf("ones")
        K.op("dve", lambda e: e.memset(ones_bf[:], 1.0), [], [B_ones])
        blk_bf = sb("blk_bf", [128, 128], BF16)
        B_blk = Buf("blk")
        K.op("dve", lambda e: e.memset(blk_bf[:], 0.0), [], [B_blk])
        K.op("dve", lambda e: e.memset(blk_bf[0:64, 0:64], 1.0), [], [B_blk])
        K.op("dve", lambda e: e.memset(blk_bf[64:128, 64:128], 1.0), [], [B_blk])

        cc_d = P.din("cc", [128, 8, 2])
        W = {}
        for l in mod_layers:
            W["wada%d" % l] = P.din("wada%d" % l, [D, 9 * D])
            W["bT%d" % l] = P.din("bT%d" % l, [128, 72])
            W["nrm%d" % l] = P.din("nrm%d" % l, [128, 3, 8])
        for l in prev_layers:
            W["wout%d" % l] = P.din("wout%d" % l, [D, D])
            W["wg2_%d" % l] = P.din("wg2_%d" % l, [D, FF])
            W["wu2_%d" % l] = P.din("wu2_%d" % l, [D, FF])
            W["wd2_%d" % l] = P.din("wd2_%d" % l, [FF, D])
        for l in proj_layers:
            W["wg1_%d" % l] = P.din("wg1_%d" % l, [D, FF])
            W["wu1_%d" % l] = P.din("wu1_%d" % l, [D, FF])
            W["wd1_%d" % l] = P.din("wd1_%d" % l, [FF, D])
            W["wqk%d" % l] = P.din("wqk%d" % l, [D, 20 * 128])
            W["wv%d" % l] = P.din("wv%d" % l, [D, 640])
            W["qkn%d" % l] = P.din("qkn%d" % l, [128, 4])
        if proj_layers:
            W["cosT"] = P.din("cosT", [128, TL])
            W["sinT"] = P.din("sinT", [128, TL])
        if has_final:
            W["nfin"] = P.din("nfin", [128, 8])
        if a_layers:
            Mw_d = P.din("Mw", [128, 8, 512], BF16)
            zmask_d = P.din("zmask", [128, 2, 1408])
            mrow_d = P.din("mrow", [128, 16, 512], BF16)
            for l in a_layers:
                W["sinkT%d" % l] = P.din("sinkT%d" % l, [128, 6])
                W["zraw%d" % l] = P.din("zraw%d" % l, [6, 128, 1408])

        S = {}
        hbuf = {}
        hT_i = P.din("hT_i", [D, TT])
        B_hin = Buf("hT_i", dram=True)
        if mode == "L1":
            l = 0
            S[0] = dict(qT=P.dtmp("qT0", [D, TT], BF16), kT=P.dtmp("kT0", [640, TT], BF16),
                        v=P.dtmp("v0", [TT, 640], BF16), XB=P.dtmp("XB0", [NR, 1024], BF16),
                        GB=nc.dram_tensor("GB0", [2 * NR, 1024], BF16, kind="Internal", addr_space="Local").ap(),
                        B_qkv=Buf("qkv0", dram=True), B_XB=Buf("XB0", dram=True), B_GB=Buf("GB0", dram=True))
            S[1] = dict(qT=P.dout("qT_o", [D, TT], BF16), kT=P.dout("kT_o", [640, TT], BF16),
                        v=P.dout("v_o", [TT, 640], BF16), XB=P.dout("XB", [NR, 1024], BF16),
                        B_qkv=Buf("qkv_o", dram=True), B_XB=Buf("XB", dram=True))
            hA = P.dtmp("hA", [D, TT]); B_hA = Buf("hA", dram=True)
            hT_o = P.dout("hT_o", [D, TT])
            h_io = [(hT_i, B_hin, hA, B_hA), (hA, B_hA, hT_o, Buf("hT_o", dram=True))]
        elif fused:
            for l in (0, 1):
                S[l] = dict(qT=P.dtmp("qT%d" % l, [D, TT], BF16), kT=P.dtmp("kT%d" % l, [640, TT], BF16),
                            v=P.dtmp("v%d" % l, [TT, 640], BF16), XB=P.dtmp("XB%d" % l, [NR, 1024], BF16),
                            GB=nc.dram_tensor("GB%d" % l, [2 * NR, 1024], BF16, kind="Internal",
                                              addr_space="Local").ap(),
                            B_qkv=Buf("qkv%d" % l, dram=True), B_XB=Buf("XB%d" % l, dram=True), B_GB=Buf("GB%d" % l, dram=True))
            hA = P.dtmp("hA", [D, TT]); hB = P.dtmp("hB", [D, TT])
            B_hA = Buf("hA", dram=True); B_hB = Buf("hB", dram=True)
            outT = P.dout("outT", [D, TL])
            h_io = [(hT_i, B_hin, hA, B_hA), (hA, B_hA, hB, B_hB), (hB, B_hB, None, None)]
        else:
            h_io = []
            if a_layers:
                l = a_layers[0]
                S[l] = dict(qT=P.din("qT_i", [D, TT], BF16), kT=P.din("kT_i", [640, TT], BF16),
                            v=P.din("v_i", [TT, 640], BF16), GB=P.din("GB", [2 * NR, 1024], BF16),
                            B_qkv=Buf("qkv_i", dram=True), B_GB=Buf("GB", dram=True))
            if has_final:
                outT = P.dout("outT", [D, TL])
                h_io.append((hT_i, B_hin, None, None))
            else:
                l = proj_layers[0]
                hT_o = P.dout("hT_o", [D, TT])
                S[l] = dict(qT=P.dout("qT_o", [D, TT], BF16), kT=P.dout("kT_o", [640, TT], BF16),
                            v=P.dout("v_o", [TT, 640], BF16),
                            XB=P.dout("XB", [NR, 1024], BF16),
                            B_qkv=Buf("qkv_o", dram=True), B_XB=Buf("XB", dram=True))
                h_io.append((hT_i, B_hin, hT_o, Buf("hT_o", dram=True)))
        if a_layers:
            yT = P.dtmp("yT", [D, TT], BF16)
            B_yT = Buf("yT", dram=True)
        B_out = Buf("outs", dram=True)

        modT = {}; modv = {}; B_mod = {}; nrm_sb = {}; bT_sb = {}
        for l in mod_layers:
            modT[l] = sb("modT%d" % l, [128, 72, 2], F32)
            modv[l] = sb("modv%d" % l, [128, 3, 3, 2, 8], F32)
            nrm_sb[l] = sb("nrm_sb%d" % l, [128, 3, 8], F32)
            bT_sb[l] = sb("bT_sb%d" % l, [128, 72], F32)
            B_mod[l] = Buf("mod%d" % l)
        cc_sb = sb("cc_sb", [128, 8, 2], F32)
        sc_sb = sb("sc_sb", [128, 8, 2], BF16)
        B_cc = Buf("cc")
        mods_done = set()

        def pack_xb(l):
            return

        def exchange(l):
            s_ = S[l]
            K._deps("pool", [s_["B_XB"]], [s_["B_GB"]])
            for pi_, (off, n) in enumerate(xb_pieces(TL)):
                key = "cc%d_%d" % (l, pi_)
                K.dsem[key] = es.enter_context(nc.semaphore("d_" + key))
                K.dcnt[key] = 1
                nc.gpsimd.collective_compute("AllGather", ALU.bypass, replica_groups=groups,
                                             ins=[s_["XB"][off:off + n, :]],
                                             outs=[s_["GB"][2 * off:2 * off + 2 * n, :]]).then_inc(K.dsem[key])
                K._mark(key, 1, [s_["B_XB"]], [s_["B_GB"]])

        def attention_phase(l):
            with_ctxq = l == 0
            s_ = S[l]
            qT_i, kT_own, v_own, GB = s_["qT"], s_["kT"], s_["v"], s_["GB"]
            B_own, B_GB = s_["B_qkv"], s_["B_GB"]
            sink_d = W["sinkT%d" % l]
            zraw_d = W["zraw%d" % l]

            def gbv(rank, nm, shape=None):
                r0, shp = XL[nm]
                return xb_view(GB, gb_row(TL, rank, r0), shape or shp)

            aes = ExitStack()
            with aes:
                def asb(name, shape, dt):
                    return aes.enter_context(nc.sbuf_tensor(name + "_%d" % l, list(shape), dt, align_bytes=256))

                NKG = (2 * TL + TC)
                kT_sb = [asb("kT_sb%d" % i, [64, NKG], BF16) for i in range(2)]
                va_sb = [asb("va_sb%d" % i, [128, NKG // 128, 128], BF16) for i in range(2)]
                B_kT = [Buf("kT%d" % i) for i in range(2)]
                B_va = [Buf("va%d" % i) for i in range(2)]
                q_sb = [asb("q_sb%d" % i, [64, 512], BF16) for i in range(2)]
                B_q = [Buf("q%d" % i) for i in range(2)]
                NPT = 6
                pT = [asb("pT%d" % i, [128, 512], BF16) for i in range(NPT)]
                B_pT = [Buf("pT%d" % i) for i in range(NPT)]
                pf = [asb("pf%d" % i, [128, 512], F32) for i in range(4)]
                B_pf = [Buf("pf%d" % i) for i in range(4)]
                y_st = [asb("y_st%d" % i, [64, 512], BF16) for i in range(2)]
                B_yst = [Buf("yst%d" % i) for i in range(2)]
                lnd = asb("lnd", [64, 512], F32)
                B_lnd = Buf("lnd")
                rden = asb("rden", [64, 512], F32)
                B_rden = Buf("rden")
                Mw_sb = asb("Mw_sb", [128, 8, 512], BF16)
                B_Mw = Buf("Mw")
                mrow_sb = asb("mrow_sb", [128, 16, 512], BF16)
                B_mrow = Buf("mrow")
                zmask_sb = asb("zmask_sb", [128, 2, 1408], F32)
                B_zmask = Buf("zmask")
                zraw_sb = asb("zraw_sb", [128, 1408], F32)
                B_zraw = Buf("zraw")
                zexp_sb = asb("zexp_sb", [128, 1408], F32)
                B_zexp = Buf("zexp")
                zfull = [asb("zfull%d" % i, [128, 1408], F32) for i in range(2)]
                zint = [asb("zint%d" % i, [128, 1408], F32) for i in range(2)]
                B_z = [Buf("z%d" % i) for i in range(2)]
                sink_sb = asb("sink_sb", [128, 6], F32)
                es_sb = asb("es_sb", [128, 6], F32)
                B_es = Buf("es")

                K.dma("sp", Mw_sb[:], Mw_d, [], [B_Mw])
                K.dma("sp", mrow_sb[:], mrow_d, [], [B_mrow])
                K.dma("sp", zmask_sb[:], zmask_d, [], [B_zmask])
                K.dma("sp", sink_sb[:], sink_d, [], [B_es])
                K.op("act", lambda e: e.activation(out=es_sb[:], in_=sink_sb[:], func=AF.Exp), [B_es], [B_es])
                for i in range(2):
                    K.op("dve", lambda e, i=i: e.memset(va_sb[i][:, :, 64:128], 1.0), [], [B_va[i]])

                NTL = TL // 128
                groups_ = []
                for g in range(2):
                    r = slice(g * 64, (g + 1) * 64)
                    kp = [(0, 128, gbv(0, "kw_last")[r, :], B_GB), (128, TL, kT_own[r, 0:TL], B_own),
                          (128 + TL, 128, gbv(1, "kw_first")[r, :], B_GB), (256 + TL, TC, kT_own[r, TL:TT], B_own)]
                    vp = [(0, 1, gbv(0, "vw_last")[:, r], B_GB), (1, NTL, v_own[0:TL, r], B_own),
                          (1 + NTL, 1, gbv(1, "vw_first")[:, r], B_GB), (2 + NTL, 2, v_own[TL:TT, r], B_own)]
                    groups_.append(("win", kp, vp, [((g * 3 + i) * 64, g * 3 + i) for i in range(3)], None))
                for g in range(2):
                    r = slice(g * 64, (g + 1) * 64)
                    r2 = slice(128 + g * 64, 128 + (g + 1) * 64)
                    kp = [(0, TL, gbv(0, "kg")[r, :], B_GB), (TL, TL, gbv(1, "kg")[r, :], B_GB),
                          (2 * TL, TC, kT_own[r2, TL:TT], B_own)]
                    vp = [(0, NTL, gbv(0, "vg")[:, r], B_GB), (NTL, NTL, gbv(1, "vg")[:, r], B_GB),
                          (2 * NTL, 2, v_own[TL:TT, r2], B_own)]
                    groups_.append(("glob", kp, vp, [(384 + (g * 2 + i) * 64, None) for i in range(2)], None))
                for h in range(6):
                    r = slice(h * 64, (h + 1) * 64)
                    r2 = slice(256 + h * 64, 256 + (h + 1) * 64)
                    j, rr = h // 2, slice((h % 2) * 64, (h % 2) * 64 + 64)
                    r0f, _ = XL["vn_first"]
                    r0l, _ = XL["vn_last"]
                    vnf = xb_view(GB, gb_row(TL, 1, r0f + j * 32), (256, 128))[:, rr]
                    vnl = xb_view(GB, gb_row(TL, 0, r0l + j * 32), (256, 128))[:, rr]
                    kp = [(0, 256, gbv(0, "kn_last")[r, :], B_GB), (256, TL, kT_own[r2, 0:TL], B_own),
                          (256 + TL, 256, gbv(1, "kn_first")[r, :], B_GB), (512 + TL, TC, kT_own[r2, TL:TT], B_own)]
                    vp = [(0, 2, vnl, B_GB), (2, NTL, v_own[0:TL, r2], B_own),
                          (2 + NTL, 2, vnf, B_GB), (4 + NTL, 2, v_own[TL:TT, r2], B_own)]
                    groups_.append(("nbr", kp, vp, [(640 + h * 64, None)], h))

                def load_group(gi):
                    kind, kp, vp, heads, zh = groups_[gi]
                    s = gi % 2
                    for (c0, ncol, src, bsrc) in kp:
                        K.dma("sp", kT_sb[s][:, c0:c0 + ncol], src, [bsrc], [B_kT[s]])
                    for (t0, ntile, src, bsrc) in vp:
                        K.dma("sp", va_sb[s][:, t0:t0 + ntile, 0:64],
                              src.rearrange("(t p) d -> p t d", p=128), [bsrc], [B_va[s]])

                state = {"qs": 0, "pt": 0, "pf": 0, "pss": 0, "pso": 0, "ys": 0, "zs": 0}

                def attend(s, q_row, q_col, nq, ktiles, sink_idx):
                    qs = state["qs"]; state["qs"] ^= 1
                    K.dma("sp", q_sb[qs][:, 0:nq], qT_i[q_row:q_row + 64, q_col:q_col + nq],
                          [B_own], [B_q[qs]])
                    po = 3 + state["pso"]; state["pso"] ^= 1
                    n = len(ktiles)
                    ptile = {}

                    def emit_qk(i):
                        kcol, vt, masks = ktiles[i]
                        b = (0, 1, 2, 5, 6)[state["pss"]]; state["pss"] = (state["pss"] + 1) % 5
                        K.op("pe", lambda e: e.matmul(ps[b][:, 0:nq], lhsT=kT_sb[s][0:64, kcol:kcol + 128],
                                                      rhs=q_sb[qs][0:64, 0:nq], start=True, stop=True),
                             [B_kT[s], B_q[qs]], [Bps[b]])
                        pt = state["pt"]; state["pt"] = (state["pt"] + 1) % NPT
                        ptile[i] = pt
                        if not masks:
                            K.op("act", lambda e: e.activation(out=pT[pt][:, 0:nq], in_=ps[b][:, 0:nq],
                                                               func=AF.Exp, scale=0.125),
                                 [Bps[b]], [B_pT[pt]])
                        else:
                            f = state["pf"]; state["pf"] = (state["pf"] + 1) % 4
                            K.op("act", lambda e: e.activation(out=pf[f][:, 0:nq], in_=ps[b][:, 0:nq],
                                                               func=AF.Exp, scale=0.125),
                                 [Bps[b]], [B_pf[f]])
                            cur = f
                            for mi, (m_ap, m_buf) in enumerate(masks):
                                lastm = mi == len(masks) - 1
                                if lastm:
                                    K.op("dve", lambda e: e.tensor_tensor(out=pT[pt][:, 0:nq], in0=pf[cur][:, 0:nq],
                                                                          in1=m_ap, op=ALU.mult),
                                         [B_pf[cur], m_buf], [B_pT[pt]])
                                else:
                                    f2 = state["pf"]; state["pf"] = (state["pf"] + 1) % 4
                                    K.op("dve", lambda e: e.tensor_tensor(out=pf[f2][:, 0:nq], in0=pf[cur][:, 0:nq],
                                                                          in1=m_ap, op=ALU.mult),
                                         [B_pf[cur], m_buf], [B_pf[f2]])
                                    cur = f2

                    def emit_pv(i):
                        kcol, vt, masks = ktiles[i]
                        pt = ptile[i]
                        K.op("pe", lambda e: e.matmul(ps[po][:, 0:nq], lhsT=va_sb[s][:, vt, :],
                                                      rhs=pT[pt][:, 0:nq], start=(i == 0), stop=(i == n - 1)),
                             [B_va[s], B_pT[pt]], [Bps[po]], inc=(i == n - 1))

                    LA = 3
                    for i in range(min(LA, n)):
                        emit_qk(i)
                    for i in range(n):
                        if i + LA < n:
                            emit_qk(i + LA)
                        emit_pv(i)
                    if sink_idx is None:
                        K.op("act", lambda e: e.activation(out=lnd[:, 0:nq], in_=ps[po][64:128, 0:nq], func=AF.Ln),
                             [Bps[po]], [B_lnd])
                    else:
                        K.op("act", lambda e: e.activation(out=lnd[:, 0:nq], in_=ps[po][64:128, 0:nq], func=AF.Ln,
                                                           bias=es_sb[64:128, sink_idx:sink_idx + 1]),
                             [Bps[po], B_es], [B_lnd])
                    K.op("act", lambda e: e.activation(out=rden[:, 0:nq], in_=lnd[:, 0:nq], func=AF.Exp, scale=-1.0),
                         [B_lnd], [B_rden])
                    ys = state["ys"]; state["ys"] ^= 1
                    K.op("dve", lambda e: e.tensor_tensor(out=y_st[ys][:, 0:nq], in0=ps[po][0:64, 0:nq],
                                                          in1=rden[:, 0:nq], op=ALU.mult),
                         [Bps[po], B_rden], [B_yst[ys]])
                    K.dma("sp", yT[q_row:q_row + 64, q_col:q_col + nq], y_st[ys][:, 0:nq],
                          [B_yst[ys]], [B_yT])

                load_group(0)
                for gi, (kind, kp, vp, heads, zh) in enumerate(groups_):
                    s = gi % 2
                    if gi + 1 < len(groups_):
                        load_group(gi + 1)
                    if kind == "nbr":
                        zs = state["zs"]; state["zs"] ^= 1
                        K.dma("sp", zraw_sb[:], zraw_d[zh], [], [B_zraw])
                        K.op("act", lambda e: e.activation(out=zexp_sb[:], in_=zraw_sb[:], func=AF.Exp),
                             [B_zraw], [B_zexp])
                        K.op("dve", lambda e: e.tensor_tensor(out=zfull[zs][:], in0=zexp_sb[:], in1=zmask_sb[:, 0, :],
                                                              op=ALU.mult), [B_zexp, B_zmask], [B_z[zs]])
                        K.op("dve", lambda e: e.tensor_tensor(out=zint[zs][:], in0=zexp_sb[:], in1=zmask_sb[:, 1, :],
                                                              op=ALU.mult), [B_zexp, B_zmask], [B_z[zs]])
                    for (q_row, sink_idx) in heads:
                        for t in range(NQT):
                            kt = []
                            if kind == "win":
                                cb = TL + 256
                                for m in range(6):
                                    blk = 4 * t + m - 1
                                    mi = m
                                    if t == 0 and m == 0:
                                        mi = 6
                                    if t == NQT - 1 and m == 5:
                                        mi = 7
                                    kt.append((128 * (blk + 1), blk + 1, [(Mw_sb[:, mi, :], B_Mw)]))
                            elif kind == "glob":
                                cb = 2 * TL
                                for i in range(2 * TL // 128):
                                    kt.append((128 * i, i, []))
                            else:
                                cb = TL + 512
                                for j in range(8):
                                    kcol = 64 * (8 * t + 2 * j)
                                    lo = (14 - 2 * j) * 64
                                    if t == 0 or t == NQT - 1:
                                        e_idx = 0 if t == 0 else 1
                                        masks = [(zfull[zs][:, lo:lo + 512], B_z[zs]),
                                                 (mrow_sb[:, e_idx * 8 + j, :], B_mrow)]
                                    else:
                                        masks = [(zint[zs][:, lo:lo + 512], B_z[zs])]
                                    kt.append((kcol, kcol // 128, masks))
                            for i in range(2):
                                kt.append((cb + 128 * i, cb // 128 + i, []))
                            attend(s, q_row, t * 512, 512, kt, sink_idx)
                        if with_ctxq:
                            cb = {"win": TL + 256, "glob": 2 * TL, "nbr": TL + 512}[kind]
                            kt = [(cb + 128 * i, cb // 128 + i, []) for i in range(2)]
                            attend(s, q_row, TL, TC, kt, sink_idx)

        def token_phase(pi, l_prev, l_next, final, hT_in, B_hTi, hT_o, B_hTo):
            tes = ExitStack()
            with tes:
                def tsb(name, shape, dt):
                    return tes.enter_context(nc.sbuf_tensor(name + "_t%d" % pi, list(shape), dt, align_bytes=256))

                h_sb = tsb("h_sb", [128, 8, NT], F32); B_h = Buf("h")
                u_sb = tsb("u_sb", [128, 8, NT], BF16); B_u = Buf("u")
                hid_sb = tsb("hid_sb", [128, NF, NT], BF16); B_hid = Buf("hid")
                wA = [(tsb("wA%d" % i, [128, 8, 512], BF16), Buf("wA%d" % i)) for i in range(4)]
                wDt = [(tsb("wD", [128, NF, D], BF16), Buf("wD"))]
                tmpf = [tsb("tmpf%d" % i, [128, 512], F32) for i in range(4)]
                B_tmpf = [Buf("tmpf%d" % i) for i in range(4)]
                rstd_sb = tsb("rstd_sb", [128, 512], F32); B_rstd = Buf("rstd")
                sq_sb = [tsb("sq%d" % i, [128, 512], BF16) for i in range(2)]
                B_sq = [Buf("sq%d" % i) for i in range(2)]
                B_cs = Buf("cs")
                B_stg = [Buf("stg%d" % i) for i in range(2)]
                B_vst = [Buf("vst%d" % i) for i in range(2)]
                if l_next is not None:
                    cos_sb = tsb("cos_sb", [128, 512], F32)
                    sin_sb = tsb("sin_sb", [128, 512], F32)
                    stg = [tsb("stg%d" % i, [128, 512], BF16) for i in range(2)]
                    vst = [tsb("vst%d" % i, [128, 640], BF16) for i in range(2)]
                ostg = [tsb("ostg%d" % i, [128, 512], F32) for i in range(2)] if final else None
                B_ostg = [Buf("ostg%d" % i) for i in range(2)]
                if l_next is not None:
                    qkn_sb = tsb("qkn_sb", [128, 4], F32)
                if final:
                    nfin_sb = tsb("nfin_sb", [128, 8], F32)
                B_small = Buf("small")
                if l_next is not None:
                    s_o = S[l_next]
                    qT_o, kT_o, v_o, B_qkvo = s_o["qT"], s_o["kT"], s_o["v"], s_o["B_qkv"]
                    XBo, B_XBo = s_o.get("XB"), s_o.get("B_XB")

                    def xbv(nm):
                        r0x, shp = XL[nm]
                        return xb_view(XBo, r0x, shp)

                ws = WStream(K, {"A": wA, "D": wDt}, "p%d" % pi)

                tiles = [(o, w, 0) for (o, w) in _blocks(TL, NT)]
                if not final:
                    tiles.append((TL, TC, 1))

                def a_load(src2d, c0, cw):
                    src = src2d.rearrange("(c p) n -> p c n", p=128)[:, :, c0:c0 + cw]
                    return [(lambda t, cw=cw: t[:, :, 0:cw], src)]

                def d_load(src2d):
                    src = src2d.rearrange("(j p) n -> p j n", p=128)
                    out = []
                    for j0 in range(0, NF, 4):
                        j1 = min(NF, j0 + 4)
                        out.append((lambda t, j0=j0, j1=j1: t[:, j0:j1, :], src[:, j0:j1, :]))
                    return out

                new_mods = [l for l in mod_layers if l not in mods_done]
                for l in new_mods:
                    for (c0, cw) in _blocks(9 * D):
                        ws.add("A", a_load(W["wada%d" % l], c0, cw))
                for (o, w, v) in tiles:
                    if l_prev is not None:
                        l = l_prev
                        for (c0, cw) in _blocks(D):
                            ws.add("A", a_load(W["wout%d" % l], c0, cw))
                        for (c0, cw) in _blocks(FF):
                            ws.add("A", a_load(W["wg2_%d" % l], c0, cw))
                            ws.add("A", a_load(W["wu2_%d" % l], c0, cw))
                        ws.add("D", d_load(W["wd2_%d" % l]))
                    if l_next is not None:
                        l = l_next
                        for (c0, cw) in _blocks(FF):
                            ws.add("A", a_load(W["wg1_%d" % l], c0, cw))
                            ws.add("A", a_load(W["wu1_%d" % l], c0, cw))
                        ws.add("D", d_load(W["wd1_%d" % l]))
                        for (c0, cw) in _blocks(20 * 128):
                            ws.add("A", a_load(W["wqk%d" % l], c0, cw))
                        for (c0, cw) in _blocks(640):
                            ws.add("A", a_load(W["wv%d" % l], c0, cw))
                ws.start()

                if new_mods:
                    K.dma("sp", cc_sb[:], cc_d, [], [B_cc])
                    K.op("act", lambda e: e.activation(out=sc_sb[:], in_=cc_sb[:], func=AF.Silu), [B_cc], [B_cc])
                for l in new_mods:
                    K.dma("sp", nrm_sb[l][:], W["nrm%d" % l], [], [B_mod[l]])
                    K.dma("sp", bT_sb[l][:], W["bT%d" % l], [], [B_mod[l]])
                if l_next is not None:
                    K.dma("sp", qkn_sb[:], W["qkn%d" % l_next], [], [B_small])
                if final:
                    K.dma("sp", nfin_sb[:], W["nfin"], [], [B_small])

                for l in new_mods:
                    mods_done.add(l)
                    for bi, (c0, cw) in enumerate(_blocks(9 * D)):
                        wt, wb = ws.take("A")
                        pb = 6 + (bi % 2)
                        for jj in range(4):
                            for c in range(8):
                                K.op("pe", lambda e: e.matmul(
                                    ps[pb][:, 2 * jj:2 * jj + 2], lhsT=wt[:, c, jj * 128:(jj + 1) * 128],
                                    rhs=sc_sb[:, c, :], start=(c == 0), stop=(c == 7)),
                                    [wb, B_cc], [Bps[pb]], inc=(c == 7))
                        ws.release("A")
                        j0 = bi * 4
                        for v in range(2):
                            K.op("dve", lambda e: e.tensor_tensor(
                                out=modT[l][:, j0:j0 + 4, v],
                                in0=ps[pb][:, 0:8].rearrange("p (j v) -> p j v", v=2)[:, :, v],
                                in1=bT_sb[l][:, j0:j0 + 4], op=ALU.add),
                                [Bps[pb], B_mod[l]], [B_mod[l]])
                    for s_ in range(3):
                        for v in range(2):
                            K.op("dve", lambda e: e.scalar_tensor_tensor(
                                out=modv[l][:, s_, 0, v, :], in0=modT[l][:, (3 * s_ + 1) * 8:(3 * s_ + 1) * 8 + 8, v],
                                scalar=1.0, in1=nrm_sb[l][:, s_, :], op0=ALU.add, op1=ALU.mult),
                                [B_mod[l]], [B_mod[l]])
                            K.op("dve", lambda e: e.tensor_copy(
                                out=modv[l][:, s_, 1, v, :], in_=modT[l][:, (3 * s_) * 8:(3 * s_) * 8 + 8, v]),
                                [B_mod[l]], [B_mod[l]])
                            gsc = 1.0 if s_ == 1 else 0.5
                            K.op("dve", lambda e: e.tensor_scalar(
                                out=modv[l][:, s_, 2, v, :], in0=modT[l][:, (3 * s_ + 2) * 8:(3 * s_ + 2) * 8 + 8, v],
                                scalar1=gsc, scalar2=None, op0=ALU.mult),
                                [B_mod[l]], [B_mod[l]])

                def mv(l, s_, k, v, c):
                    return modv[l][:, s_, k, v, c:c + 1]

                st = {"tf": 0, "sq": 0, "pg": 0, "pu": 0, "pd": 0, "stg": 0, "vst": 0, "ostg": 0}

                def rstd_for(src_fn, nchunk, w, lhs_ones, b_ones, inv_n, psb):
                    for c in range(nchunk):
                        q = st["sq"]; st["sq"] ^= 1
                        ap, bf = src_fn(c)
                        K.op("act", lambda e: e.activation(out=sq_sb[q][:, 0:w], in_=ap, func=AF.Square),
                             [bf], [B_sq[q]])
                        K.op("pe", lambda e: e.matmul(ps[psb][:, 0:w], lhsT=lhs_ones, rhs=sq_sb[q][:, 0:w],
                                                      start=(c == 0), stop=(c == nchunk - 1)),
                             [B_sq[q], b_ones], [Bps[psb]], inc=True)
                    K.op("act", lambda e: e.activation(out=rstd_sb[:, 0:w], in_=ps[psb][:, 0:w], func=AF.Ln,
                                                       scale=inv_n, bias=EPS),
                         [Bps[psb]], [B_rstd])
                    K.op("act", lambda e: e.activation(out=rstd_sb[:, 0:w], in_=rstd_sb[:, 0:w], func=AF.Exp,
                                                       scale=-0.5),
                         [B_rstd], [B_rstd])

                def norm_mod(l, s_, v, ntok):
                    for (o, w) in _blocks(ntok):
                        rstd_for(lambda c: (h_sb[:, c, o:o + w], B_h), 8, w, ones_bf[:], B_ones, 1.0 / D, 4)
                        for c in range(8):
                            f = st["tf"]; st["tf"] = (st["tf"] + 1) % 4
                            K.op("dve", lambda e: e.tensor_tensor(out=tmpf[f][:, 0:w], in0=h_sb[:, c, o:o + w],
                                                                  in1=rstd_sb[:, 0:w], op=ALU.mult),
                                 [B_h, B_rstd], [B_tmpf[f]])
                            K.op("act", lambda e: e.activation(out=u_sb[:, c, o:o + w], in_=tmpf[f][:, 0:w],
                                                               func=AF.Identity, scale=mv(l, s_, 0, v, c),
                                                               bias=mv(l, s_, 1, v, c)),
                                 [B_tmpf[f], B_mod[l]], [B_u])

                def ffn(l, s_, v, ntok):
                    norm_mod(l, s_, v, ntok)
                    subs = _blocks(ntok)
                    for (c0, cw) in _blocks(FF):
                        gt, gb = ws.take("A")
                        ut, ub = ws.take("A")
                        for jj in range(cw // 128):
                            j = c0 // 128 + jj
                            for (o, w) in subs:
                                pg = st["pg"]; st["pg"] ^= 1
                                pu = 2 + st["pu"]; st["pu"] ^= 1
                                for c in range(8):
                                    K.op("pe", lambda e: e.matmul(ps[pg][:, 0:w], lhsT=gt[:, c, jj * 128:(jj + 1) * 128],
                                                                  rhs=u_sb[:, c, o:o + w], start=(c == 0), stop=(c == 7)),
                                         [gb, B_u], [Bps[pg]], inc=(c == 7))
                                for c in range(8):
                                    K.op("pe", lambda e: e.matmul(ps[pu][:, 0:w], lhsT=ut[:, c, jj * 128:(jj + 1) * 128],
                                                                  rhs=u_sb[:, c, o:o + w], start=(c == 0), stop=(c == 7)),
                                         [ub, B_u], [Bps[pu]], inc=(c == 7))
                                f = st["tf"]; st["tf"] = (st["tf"] + 1) % 4
                                K.op("act", lambda e: e.activation(out=tmpf[f][:, 0:w], in_=ps[pg][:, 0:w], func=AF.Silu),
                                     [Bps[pg]], [B_tmpf[f]])
                                K.op("dve", lambda e: e.tensor_tensor(out=hid_sb[:, j, o:o + w], in0=tmpf[f][:, 0:w],
                                                                      in1=ps[pu][:, 0:w], op=ALU.mult),
                                     [B_tmpf[f], Bps[pu]], [B_hid])
                        ws.release("A")
                        ws.release("A")
                    dt_, db = ws.take("D")
                    for d in range(8):
                        for (o, w) in subs:
                            pd = 4 + st["pd"]; st["pd"] ^= 1
                            for j in range(NF):
                                K.op("pe", lambda e: e.matmul(ps[pd][:, 0:w], lhsT=dt_[:, j, d * 128:(d + 1) * 128],
                                                              rhs=hid_sb[:, j, o:o + w], start=(j == 0), stop=(j == NF - 1)),
                                     [db, B_hid], [Bps[pd]], inc=(j == NF - 1))
                            K.op("dve", lambda e: e.scalar_tensor_tensor(
                                out=h_sb[:, d, o:o + w], in0=ps[pd][:, 0:w], scalar=mv(l, s_, 2, v, d),
                                in1=h_sb[:, d, o:o + w], op0=ALU.mult, op1=ALU.add),
                                [Bps[pd], B_mod[l], B_h], [B_h])
                    ws.release("D")

                def wout_step(l, v, tok0, ntok):
                    K.dma("sp", hid_sb[:, 0:8, 0:ntok],
                          yT.rearrange("(c p) n -> p c n", p=128)[:, :, tok0:tok0 + ntok], [B_yT], [B_hid])
                    subs = _blocks(ntok)
                    for (c0, cw) in _blocks(D):
                        wt, wb = ws.take("A")
                        for dd in range(4):
                            d = c0 // 128 + dd
                            for (o, w) in subs:
                                pd = 4 + st["pd"]; st["pd"] ^= 1
                                for c in range(8):
                                    K.op("pe", lambda e: e.matmul(ps[pd][:, 0:w], lhsT=wt[:, c, dd * 128:(dd + 1) * 128],
                                                                  rhs=hid_sb[:, c, o:o + w], start=(c == 0), stop=(c == 7)),
                                         [wb, B_hid], [Bps[pd]], inc=(c == 7))
                                K.op("dve", lambda e: e.scalar_tensor_tensor(
                                    out=h_sb[:, d, o:o + w], in0=ps[pd][:, 0:w], scalar=mv(l, 1, 2, v, d),
                                    in1=h_sb[:, d, o:o + w], op0=ALU.mult, op1=ALU.add),
                                    [Bps[pd], B_mod[l], B_h], [B_h])
                        ws.release("A")

                def proj_step(l, v, tok0, ntok):
                    norm_mod(l, 1, v, ntok)
                    subs = _blocks(ntok)
                    latent = v == 0
                    chunk_plan = []
                    cidx = 0
                    for (kind, dest, row) in QK_ITEMS:
                        if kind == "plain":
                            chunk_plan.append((kind, dest, row, cidx, None)); cidx += 1
                        else:
                            chunk_plan.append((kind, dest, row, cidx, cidx + 1)); cidx += 2
                    assert cidx == 20
                    cur_blk = -1
                    wt = wb = None
                    for (kind, dest, row, cx, cs_) in chunk_plan:
                        blk = cx // 4
                        if blk != cur_blk:
                            if cur_blk >= 0:
                                ws.release("A")
                            wt, wb = ws.take("A")
                            cur_blk = blk
                        lx = (cx % 4) * 128
                        for (o, w) in subs:
                            px = 0 + st["pg"]; st["pg"] ^= 1
                            for c in range(8):
                                K.op("pe", lambda e: e.matmul(ps[px][:, 0:w], lhsT=wt[:, c, lx:lx + 128],
                                                              rhs=u_sb[:, c, o:o + w], start=(c == 0), stop=(c == 7)),
                                     [wb, B_u], [Bps[px]], inc=(c == 7))
                            if cs_ is not None and latent:
                                ls = (cs_ % 4) * 128
                                pxs = 2 + st["pu"]; st["pu"] ^= 1
                                for c in range(8):
                                    K.op("pe", lambda e: e.matmul(ps[pxs][:, 0:w], lhsT=wt[:, c, ls:ls + 128],
                                                                  rhs=u_sb[:, c, o:o + w], start=(c == 0), stop=(c == 7)),
                                         [wb, B_u], [Bps[pxs]], inc=(c == 7))
                            sg = st["stg"]; st["stg"] ^= 1
                            if kind != "plain" and latent:
                                K.dma("sp", cos_sb[:, 0:w], W["cosT"][:, tok0 + o:tok0 + o + w], [], [B_cs])
                                K.dma("sp", sin_sb[:, 0:w], W["sinT"][:, tok0 + o:tok0 + o + w], [], [B_cs])
                            if kind == "nrope":
                                rstd_for(lambda c: (ps[px][:, 0:w], Bps[px]), 1, w, blk_bf[:], B_blk, 1.0 / 64, 6)
                                gcol = 0 if dest == "q" else 2
                            if kind == "plain" or (kind == "rope" and not latent):
                                K.op("act", lambda e: e.activation(out=stg[sg][:, 0:w], in_=ps[px][:, 0:w], func=AF.Copy),
                                     [Bps[px]], [B_stg[sg]])
                            elif kind == "rope":
                                f1 = st["tf"]; st["tf"] = (st["tf"] + 1) % 4
                                f2 = st["tf"]; st["tf"] = (st["tf"] + 1) % 4
                                K.op("dve", lambda e: e.tensor_tensor(out=tmpf[f1][:, 0:w], in0=ps[px][:, 0:w],
                                                                      in1=cos_sb[:, 0:w], op=ALU.mult),
                                     [Bps[px], B_cs], [B_tmpf[f1]])
                                K.op("dve", lambda e: e.tensor_tensor(out=tmpf[f2][:, 0:w], in0=ps[pxs][:, 0:w],
                                                                      in1=sin_sb[:, 0:w], op=ALU.mult),
                                     [Bps[pxs], B_cs], [B_tmpf[f2]])
                                K.op("dve", lambda e: e.tensor_tensor(out=stg[sg][:, 0:w], in0=tmpf[f1][:, 0:w],
                                                                      in1=tmpf[f2][:, 0:w], op=ALU.add),
                                     [B_tmpf[f1], B_tmpf[f2]], [B_stg[sg]])
                            elif kind == "nrope" and latent:
                                f1 = st["tf"]; st["tf"] = (st["tf"] + 1) % 4
                                f2 = st["tf"]; st["tf"] = (st["tf"] + 1) % 4
                                K.op("dve", lambda e: e.scalar_tensor_tensor(
                                    out=tmpf[f1][:, 0:w], in0=ps[px][:, 0:w], scalar=qkn_sb[:, gcol:gcol + 1],
                                    in1=cos_sb[:, 0:w], op0=ALU.mult, op1=ALU.mult),
                                    [Bps[px], B_cs, B_small], [B_tmpf[f1]])
                                K.op("dve", lambda e: e.scalar_tensor_tensor(
                                    out=tmpf[f2][:, 0:w], in0=ps[pxs][:, 0:w], scalar=qkn_sb[:, gcol + 1:gcol + 2],
                                    in1=sin_sb[:, 0:w], op0=ALU.mult, op1=ALU.mult),
                                    [Bps[pxs], B_cs, B_small], [B_tmpf[f2]])
                                K.op("dve", lambda e: e.tensor_tensor(out=tmpf[f1][:, 0:w], in0=tmpf[f1][:, 0:w],
                                                                      in1=tmpf[f2][:, 0:w], op=ALU.add),
                                     [B_tmpf[f1], B_tmpf[f2]], [B_tmpf[f1]])
                                K.op("dve", lambda e: e.tensor_tensor(out=stg[sg][:, 0:w], in0=tmpf[f1][:, 0:w],
                                                                      in1=rstd_sb[:, 0:w], op=ALU.mult),
                                     [B_tmpf[f1], B_rstd], [B_stg[sg]])
                            else:
                                K.op("dve", lambda e: e.scalar_tensor_tensor(
                                    out=stg[sg][:, 0:w], in0=ps[px][:, 0:w], scalar=qkn_sb[:, gcol:gcol + 1],
                                    in1=rstd_sb[:, 0:w], op0=ALU.mult, op1=ALU.mult),
                                    [Bps[px], B_rstd, B_small], [B_stg[sg]])
                            dst = qT_o if dest == "q" else kT_o
                            K.dma("sp", dst[row:row + 128, tok0 + o:tok0 + o + w], stg[sg][:, 0:w],
                                  [B_stg[sg]], [B_qkvo])
                            if latent and dest == "k" and XBo is not None:
                                g0 = tok0 + o
                                if row == 128:
                                    K.dma("sp", xbv("kg")[:, g0:g0 + w], stg[sg][:, 0:w],
                                          [B_stg[sg]], [B_XBo])
                                elif row == 0:
                                    if g0 == 0:
                                        K.dma("sp", xbv("kw_first"), stg[sg][:, 0:128],
                                              [B_stg[sg]], [B_XBo])
                                    if g0 + w == TL:
                                        K.dma("sp", xbv("kw_last"), stg[sg][:, w - 128:w],
                                              [B_stg[sg]], [B_XBo])
                                else:
                                    jn = (row - 256) // 128
                                    if g0 == 0:
                                        K.dma("sp", xbv("kn_first")[jn * 128:(jn + 1) * 128, :], stg[sg][:, 0:256],
                                              [B_stg[sg]], [B_XBo])
                                    if g0 + w == TL:
                                        K.dma("sp", xbv("kn_last")[jn * 128:(jn + 1) * 128, :], stg[sg][:, w - 256:w],
                                              [B_stg[sg]], [B_XBo])
                    ws.release("A")
                    v0t, v0b = ws.take("A")
                    v1t, v1b = ws.take("A")
                    for tb in range(ntok // 128):
                        pa = 0 + st["pg"]; st["pg"] ^= 1
                        pb_ = 2 + st["pu"]; st["pu"] ^= 1
                        for c in range(8):
                            K.op("pe", lambda e: e.matmul(ps[pa][:, 0:512], lhsT=u_sb[:, c, tb * 128:(tb + 1) * 128],
                                                          rhs=v0t[:, c, 0:512], start=(c == 0), stop=(c == 7)),
                                 [v0b, B_u], [Bps[pa]], inc=(c == 7))
                        for c in range(8):
                            K.op("pe", lambda e: e.matmul(ps[pb_][:, 0:128], lhsT=u_sb[:, c, tb * 128:(tb + 1) * 128],
                                                          rhs=v1t[:, c, 0:128], start=(c == 0), stop=(c == 7)),
                                 [v1b, B_u], [Bps[pb_]], inc=(c == 7))
                        vs_ = st["vst"]; st["vst"] ^= 1
                        K.op("act", lambda e: e.activation(out=vst[vs_][:, 0:512], in_=ps[pa][:, 0:512], func=AF.Copy),
                             [Bps[pa]], [B_vst[vs_]])
                        K.op("dve", lambda e: e.tensor_copy(out=vst[vs_][:, 512:640], in_=ps[pb_][:, 0:128]),
                             [Bps[pb_]], [B_vst[vs_]])
                        K.dma("sp", v_o[tok0 + tb * 128:tok0 + (tb + 1) * 128, :], vst[vs_][:],
                              [B_vst[vs_]], [B_qkvo])
                        if latent and XBo is not None:
                            t0 = tok0 + tb * 128
                            K.dma("sp", xbv("vg")[t0:t0 + 128, :], vst[vs_][:, 128:256],
                                  [B_vst[vs_]], [B_XBo])
                            if t0 == 0:
                                K.dma("sp", xbv("vw_first"), vst[vs_][:, 0:128], [B_vst[vs_]], [B_XBo])
                            if t0 == TL - 128:
                                K.dma("sp", xbv("vw_last"), vst[vs_][:, 0:128], [B_vst[vs_]], [B_XBo])
                            for (nm, base) in (("vn_first", 0), ("vn_last", TL - 256)):
                                if base <= t0 < base + 256:
                                    r0x, _ = XL[nm]
                                    for jn in range(3):
                                        K.dma("sp", xb_view(XBo, r0x + jn * 32, (256, 128))[t0 - base:t0 - base + 128, :],
                                              vst[vs_][:, 256 + jn * 128:256 + (jn + 1) * 128],
                                              [B_vst[vs_]], [B_XBo])
                    ws.release("A")
                    ws.release("A")

                def final_step(tok0, ntok):
                    for (o, w) in _blocks(ntok):
                        rstd_for(lambda c: (h_sb[:, c, o:o + w], B_h), 8, w, ones_bf[:], B_ones, 1.0 / D, 4)
                        for c in range(8):
                            og = st["ostg"]; st["ostg"] ^= 1
                            K.op("dve", lambda e: e.scalar_tensor_tensor(
                                out=ostg[og][:, 0:w], in0=h_sb[:, c, o:o + w], scalar=nfin_sb[:, c:c + 1],
                                in1=rstd_sb[:, 0:w], op0=ALU.mult, op1=ALU.mult),
                                [B_h, B_rstd, B_small], [B_ostg[og]])
                            K.dma("sp", outT[c * 128:(c + 1) * 128, tok0 + o:tok0 + o + w], ostg[og][:, 0:w],
                                  [B_ostg[og]], [B_out])

                for (tok0, ntok, v) in tiles:
                    K.dma("sp", h_sb[:, :, 0:ntok],
                          hT_in.rearrange("(c p) n -> p c n", p=128)[:, :, tok0:tok0 + ntok], [B_hTi], [B_h])
                    if l_prev is not None:
                        wout_step(l_prev, v, tok0, ntok)
                        ffn(l_prev, 2, v, ntok)
                    if l_next is not None:
                        ffn(l_next, 0, v, ntok)
                        proj_step(l_next, v, tok0, ntok)
                    if final:
                        final_step(tok0, ntok)
                    else:
                        K.dma("sp", hT_o.rearrange("(c p) n -> p c n", p=128)[:, :, tok0:tok0 + ntok],
                              h_sb[:, :, 0:ntok], [B_h], [B_hTo])

        ti = 0
        for pi, ph in enumerate(phases):
            K.barrier()
            if ph[0] == "T":
                hin, bhin, hout, bhout = h_io[ti]
                ti += 1
                token_phase(pi, ph[1], ph[2], ph[3], hin, bhin, hout, bhout)
                if ph[2] is not None and S[ph[2]].get("XB") is not None:
                    pack_xb(ph[2])
            elif ph[0] == "X":
                exchange(ph[1])
            else:
                attention_phase(ph[1])
        K.finish()
        P.ninstr = K.ninstr
    return P


def _fm(vec):
    return np.ascontiguousarray(np.asarray(vec, np.float32).reshape(8, 128).T)


def _swap_cols(w):
    n = w.shape[1]
    idx = np.arange(n).reshape(n // 64, 2, 32)[:, ::-1, :].reshape(-1)
    return w[:, idx]


def _prep_layer(inp, l):
    w_in = np.asarray(inp["w_in"][l], np.float32)
    cols = {
        "q_w": (0, 384), "q_g": (384, 640), "q_n": (640, 1024),
        "k_w": (1024, 1152), "v_w": (1152, 1280), "k_g": (1280, 1408),
        "v_g": (1408, 1536), "k_n": (1536, 1920), "v_n": (1920, 2304),
    }

    def chunk(name, i):
        a, _ = cols[name]
        return w_in[:, a + i * 128:a + (i + 1) * 128]

    parts = []
    for nm, i in [("q_w", 0), ("q_w", 1), ("q_w", 2), ("q_g", 0), ("q_g", 1), ("k_w", 0), ("k_g", 0)]:
        c = chunk(nm, i)
        parts += [c, _swap_cols(c)]
    for nm, i in [("q_n", 0), ("q_n", 1), ("q_n", 2), ("k_n", 0), ("k_n", 1), ("k_n", 2)]:
        parts.append(chunk(nm, i))
    wqk = np.ascontiguousarray(np.concatenate(parts, axis=1))
    wv = np.ascontiguousarray(np.concatenate(
        [w_in[:, slice(*cols["v_w"])], w_in[:, slice(*cols["v_g"])], w_in[:, slice(*cols["v_n"])]], axis=1))
    p64 = np.arange(128) % 64
    qg = np.asarray(inp["q_norm_glob"][l], np.float32)
    kg = np.asarray(inp["k_norm_glob"][l], np.float32)
    qkn = np.stack([qg[p64], qg[(p64 + 32) % 64], kg[p64], kg[(p64 + 32) % 64]], axis=1).astype(np.float32)
    nrm = np.stack([_fm(inp["norm_ffn1"][l]), _fm(inp["norm_mix"][l]), _fm(inp["norm_ffn2"][l])], axis=1)
    bT = np.ascontiguousarray(np.asarray(inp["b_ada"][l], np.float32).reshape(72, 128).T)
    rpb = np.asarray(inp["rpb_nbr"][l], np.float32)
    kr = np.arange(2)[:, None, None, None]
    kc = np.arange(64)[None, :, None, None]
    ii = np.arange(22)[None, None, :, None]
    qc = np.arange(64)[None, None, None, :]
    a = 17 - ii + kr + 0 * kc + 0 * qc
    dc = kc - qc + 15 + 0 * ii + 0 * kr
    ok = (a >= 0) & (a <= 14) & (dc >= 0) & (dc <= 30)
    ac = np.clip(a, 0, 14)
    dcc = np.clip(dc, 0, 30)
    zraw = np.where(ok[None], rpb[:, ac, dcc], 0.0).astype(np.float32).reshape(6, 128, 22 * 64)
    return dict(
        wada=np.ascontiguousarray(np.asarray(inp["w_ada"][l], np.float32)), bT=bT, nrm=np.ascontiguousarray(nrm),
        wg1=np.ascontiguousarray(inp["w_ffn1_gate"][l]), wu1=np.ascontiguousarray(inp["w_ffn1_up"][l]),
        wd1=np.ascontiguousarray(inp["w_ffn1_down"][l]),
        wg2=np.ascontiguousarray(inp["w_ffn2_gate"][l]), wu2=np.ascontiguousarray(inp["w_ffn2_up"][l]),
        wd2=np.ascontiguousarray(inp["w_ffn2_down"][l]),
        wout=np.ascontiguousarray(inp["w_out"][l]), wqk=wqk, wv=wv, qkn=np.ascontiguousarray(qkn),
        sinkT=np.ascontiguousarray(np.broadcast_to(np.asarray(inp["sink_win"][l], np.float32)[None, :], (128, 6))),
        zraw=np.ascontiguousarray(zraw),
    )


def _const_tables(TL, half):
    RT = TL // GRID_W
    NQT = TL // 512
    rows_total = 2 * RT
    t = half * TL + np.arange(TL)
    row = (t // GRID_W).astype(np.float32)
    col = (t % GRID_W).astype(np.float32)
    inv = (np.float32(10000.0) ** (-np.arange(16, dtype=np.float32) / np.float32(16))).astype(np.float32)
    ang = np.concatenate([row[:, None] * inv, col[:, None] * inv], axis=-1).astype(np.float32)
    p64 = np.arange(128) % 64
    cosT = np.cos(ang)[:, p64 % 32].T.astype(np.float32)
    sgn = np.where(p64 < 32, -1.0, 1.0).astype(np.float32)
    sinT = (np.sin(ang)[:, p64 % 32].T * sgn[:, None]).astype(np.float32)
    a = np.arange(128)[:, None, None]
    i = np.arange(4)[None, :, None]
    b = np.arange(128)[None, None, :]
    Mw = np.zeros((128, 8, 512), np.float32)
    for m in range(6):
        j = m - 1
        dji = j - i
        ok = (dji == 0) | ((dji == 1) & (a <= b)) | ((dji == -1) & (a >= b))
        Mw[:, m, :] = np.broadcast_to(ok, (128, 4, 128)).reshape(128, 512)
    Mw[:, 6, :] = Mw[:, 0, :] * (1.0 if half == 1 else 0.0)
    Mw[:, 7, :] = Mw[:, 5, :] * (1.0 if half == 0 else 0.0)
    kr = np.arange(2)[:, None, None, None]
    kc = np.arange(64)[None, :, None, None]
    ii = np.arange(22)[None, None, :, None]
    qc = np.arange(64)[None, None, None, :]
    aa = 17 - ii + kr + 0 * kc + 0 * qc
    dc = kc - qc + 15 + 0 * ii + 0 * kr
    cs = np.clip(qc - 8, 0, GRID_W - 16)
    colok = (kc >= cs) & (kc < cs + 16) & (dc >= 0) & (dc <= 30) + 0 * ii + 0 * kr
    colok = np.broadcast_to(colok, (2, 64, 22, 64))
    zfull = ((aa >= 0) & (aa <= 14) & colok).reshape(128, 22 * 64)
    zint = ((aa >= 3) & (aa <= 10) & colok).reshape(128, 22 * 64)
    zmask = np.stack([zfull, zint], axis=1).astype(np.float32)
    mrow = np.zeros((128, 16, 512), np.float32)
    krr = np.arange(2)[:, None, None, None]
    qr = np.arange(8)[None, None, :, None]
    for e, tq in enumerate([0, NQT - 1]):
        for j in range(8):
            Rg = half * RT + 8 * tq - 4 + 2 * j + krr
            rg = half * RT + 8 * tq + qr
            rs = np.clip(rg - 4, 0, rows_total - 8)
            ok = (Rg >= rs) & (Rg < rs + 8)
            mrow[:, e * 8 + j, :] = np.broadcast_to(ok, (2, 64, 8, 64)).reshape(128, 512)
    return dict(cosT=np.ascontiguousarray(cosT), sinT=np.ascontiguousarray(sinT), Mw=Mw.astype(NPBF),
                zmask=np.ascontiguousarray(zmask), mrow=mrow.astype(NPBF))


_PROGS = {}


def _prog(mode, TL, groups):
    key = (mode, TL, str(groups))
    if key not in _PROGS:
        _PROGS[key] = build(mode, TL, groups)
    return _PROGS[key]


def kernel(TL=4096, nbatch=4, **inp):
    inp = {k: np.asarray(v) for k, v in inp.items()}
    x, c, ctx, c_ctx = inp["x"], inp["c"], inp["ctx"], inp["c_ctx"]
    ncore = 2 * nbatch
    groups = [[2 * b, 2 * b + 1] for b in range(nbatch)]
    L = [_prep_layer(inp, l) for l in range(2)]
    tabs = [_const_tables(TL, h) for h in range(2)]
    nfin = _fm(inp["norm_final"])
    maps = []
    for cid in range(ncore):
        b, half = cid // 2, cid % 2
        xT = np.concatenate([x[b, half * TL:(half + 1) * TL, :].T, ctx[b].T], axis=1).astype(np.float32)
        m = dict(hT_i=np.ascontiguousarray(xT),
                 cc=np.ascontiguousarray(np.stack([_fm(c[b]), _fm(c_ctx)], axis=-1)),
                 cosT=tabs[half]["cosT"], sinT=tabs[half]["sinT"], nfin=nfin,
                 Mw=tabs[half]["Mw"], zmask=tabs[half]["zmask"], mrow=tabs[half]["mrow"])
        for l in range(2):
            Ll = L[l]
            m.update({"wada%d" % l: Ll["wada"], "bT%d" % l: Ll["bT"], "nrm%d" % l: Ll["nrm"],
                      "wout%d" % l: Ll["wout"], "wg2_%d" % l: Ll["wg2"], "wu2_%d" % l: Ll["wu2"],
                      "wd2_%d" % l: Ll["wd2"], "wg1_%d" % l: Ll["wg1"], "wu1_%d" % l: Ll["wu1"],
                      "wd1_%d" % l: Ll["wd1"], "wqk%d" % l: Ll["wqk"], "wv%d" % l: Ll["wv"],
                      "qkn%d" % l: Ll["qkn"], "sinkT%d" % l: Ll["sinkT"], "zraw%d" % l: Ll["zraw"]})
        maps.append(m)
    P = _prog("FUSED", TL, groups)
    res = run_bass_kernel_spmd(P.nc, maps, core_ids=list(range(ncore)))
    out = np.zeros((nbatch, 2 * TL, D), np.float32)
    for cid in range(ncore):
        b, half = cid // 2, cid % 2
        out[b, half * TL:(half + 1) * TL, :] = np.asarray(res.results[cid]["outT"], np.float32).T
    return out
```

```python
import numpy as np
import ml_dtypes
from contextlib import ExitStack
import concourse.bass as bass
import concourse.mybir as mybir
from concourse.bass_utils import run_bass_kernel_spmd

F32 = mybir.dt.float32
BF16 = mybir.dt.bfloat16
AF = mybir.ActivationFunctionType
ALU = mybir.AluOpType
NPBF = ml_dtypes.bfloat16

D = 1024
FF = 2816
NF = 22
TC = 256
EPS = 1e-6
GRID_W = 64
NT = 1024


class Buf:
    __slots__ = ("name", "w", "r", "dram")

    def __init__(self, name, dram=False):
        self.name = name
        self.w = {}
        self.r = {}
        self.dram = dram


class KB:
    def __init__(self, nc, es):
        self.nc = nc
        self.es = es
        self.eng = {"pe": nc.tensor, "act": nc.scalar, "dve": nc.vector,
                    "pool": nc.gpsimd, "sp": nc.sync}
        self.sem = {}
        self.cnt = {}
        self.seen = {e: {} for e in self.eng}
        for e in self.eng:
            self.sem[e] = es.enter_context(nc.semaphore("s_" + e))
            self.cnt[e] = 0
        self.dsem = {}
        self.dcnt = {}
        self.ninstr = 0
        self.pending = None

    def _wait(self, e, key, val):
        if key == e and e in ("pe", "sp"):
            return
        if self.seen[e].get(key, 0) >= val:
            return
        sem = self.sem[key] if key in self.sem else self.dsem[key]
        if self.pending is not None:
            self.pending = [p for p in self.pending if p[0] != key]
            self.pending.append((key, sem, val))
        else:
            self.eng[e].wait_ge(sem, val)
        self.seen[e][key] = val

    def _deps(self, e, reads, writes, own_key=None):
        for b in reads:
            for k, v in b.w.items():
                self._wait(e, k, v)
        for b in writes:
            if not b.dram:
                for k, v in b.w.items():
                    if k == own_key:
                        continue
                    self._wait(e, k, v)
            for k, v in b.r.items():
                self._wait(e, k, v)

    def _mark(self, key, val, reads, writes):
        for b in reads:
            if b.r.get(key, 0) < val:
                b.r[key] = val
        for b in writes:
            if b.dram:
                b.w[key] = val
            else:
                b.w = {key: val}
                b.r = {}

    def op(self, e, fn, reads=(), writes=(), inc=True):
        self.pending = []
        self._deps(e, reads, writes)
        pend, self.pending = self.pending, None
        for (key, sem, val) in pend[:-1]:
            self.eng[e].wait_ge(sem, val)
        ins = fn(self.eng[e])
        if pend:
            ins._wait_ge(pend[-1][1], pend[-1][2])
        self.ninstr += 1
        if inc:
            self.cnt[e] += 1
            ins.then_inc(self.sem[e], 1)
            val = self.cnt[e]
        else:
            val = self.cnt[e] + 1
        self._mark(e, val, reads, writes)

    def dma(self, q, out, in_, reads, writes, key=None):
        if key is None:
            sbw = [b for b in writes if not b.dram]
            sbr = [b for b in reads if not b.dram]
            key = ("ld" + sbw[0].name) if sbw else ("st" + sbr[0].name)
        self._deps(q, reads, writes, own_key=key)
        if key not in self.dsem:
            self.dsem[key] = self.es.enter_context(self.nc.semaphore("d_" + key))
            self.dcnt[key] = 0
        self.dcnt[key] += 16
        self.eng[q].dma_start(out=out, in_=in_).then_inc(self.dsem[key], 16)
        self.ninstr += 1
        self._mark(key, self.dcnt[key], reads, writes)

    def barrier(self):
        for e in self.eng:
            for o in self.eng:
                if o != e and self.cnt[o] > 0:
                    self._wait(e, o, self.cnt[o])
            for key, val in self.dcnt.items():
                self._wait(e, key, val)

    def finish(self):
        for key, val in self.dcnt.items():
            self._wait("sp", key, val)


class WStream:
    def __init__(self, K, rings, tag=""):
        self.K = K
        self.tag = tag
        self.rings = rings
        self.plan = {k: [] for k in rings}
        self.emitted = {k: 0 for k in rings}
        self.released = {k: 0 for k in rings}
        self.taken = {k: 0 for k in rings}

    def add(self, kind, loads):
        self.plan[kind].append(loads)

    def pump(self, kind):
        ring = self.rings[kind]
        while (self.emitted[kind] < self.released[kind] + len(ring)
               and self.emitted[kind] < len(self.plan[kind])):
            i = self.emitted[kind]
            t, b = ring[i % len(ring)]
            for dst_fn, src in self.plan[kind][i]:
                self.K.dma("pool", dst_fn(t), src, [], [b], "w%s%s%d" % (self.tag, kind, i % len(ring)))
            self.emitted[kind] += 1

    def start(self):
        for k in self.rings:
            self.pump(k)

    def take(self, kind):
        i = self.taken[kind]
        assert i < self.emitted[kind], (kind, i)
        self.taken[kind] += 1
        ring = self.rings[kind]
        return ring[i % len(ring)]

    def release(self, kind):
        self.released[kind] += 1
        self.pump(kind)


def _blocks(n, w=512):
    return [(o, min(w, n - o)) for o in range(0, n, w)]


QK_ITEMS = [
    ("rope", "q", 0), ("rope", "q", 128), ("rope", "q", 256),
    ("nrope", "q", 384), ("nrope", "q", 512),
    ("rope", "k", 0),
    ("nrope", "k", 128),
    ("plain", "q", 640), ("plain", "q", 768), ("plain", "q", 896),
    ("plain", "k", 256), ("plain", "k", 384), ("plain", "k", 512),
]


class Prog:
    def __init__(self, mode, TL):
        self.mode = mode
        self.TL = TL
        self.TT = TL + TC
        self.nc = bass.Bass("TRN2", target_bir_lowering=False)
        self.dram = {}

    def din(self, name, shape, dt=F32):
        self.dram[name] = self.nc.dram_tensor(name, list(shape), dt, kind="ExternalInput").ap()
        return self.dram[name]

    def dout(self, name, shape, dt=F32):
        self.dram[name] = self.nc.dram_tensor(name, list(shape), dt, kind="ExternalOutput").ap()
        return self.dram[name]

    def dtmp(self, name, shape, dt=F32):
        self.dram[name] = self.nc.dram_tensor(name, list(shape), dt, kind="Internal").ap()
        return self.dram[name]


PHASES = {
    "T0": [("T", None, 0, False)],
    "A0T1": [("A", 0), ("T", 0, 1, False)],
    "A1T2": [("A", 1), ("T", 1, None, True)],
    "FUSED": [("T", None, 0, False), ("X", 0), ("A", 0), ("T", 0, 1, False), ("X", 1), ("A", 1),
              ("T", 1, None, True)],
    "L1": [("T", None, 0, False), ("X", 0), ("A", 0), ("T", 0, 1, False)],
}


def xb_layout(TL):
    pieces = [("kg", [128, TL]), ("vg", [TL, 128]),
              ("kw_first", [128, 128]), ("kw_last", [128, 128]),
              ("vw_first", [128, 128]), ("vw_last", [128, 128]),
              ("kn_first", [384, 256]), ("kn_last", [384, 256]),
              ("vn_first", [256, 384]), ("vn_last", [256, 384])]
    lay = {}
    r = 0
    for nm, (a, b) in pieces:
        n = a * b // 1024
        lay[nm] = (r, (a, b))
        r += n
    return lay, r


def xb_view(buf, r0, shape):
    a, b = shape
    n = a * b // 1024
    rows = buf[r0:r0 + n, :]
    if b >= 1024:
        return rows.rearrange("(a f) c -> a (f c)", f=b // 1024)
    if 1024 % b == 0:
        return rows.rearrange("r (e d) -> (r e) d", d=b)
    raise ValueError(shape)


def xb_pieces(TL):
    lay, NR = xb_layout(TL)
    a = lay["vg"][0]
    b = lay["kw_first"][0]
    return [(0, a), (a, b - a), (b, NR - b)]


def gb_row(TL, rank, r0):
    for (off, n) in xb_pieces(TL):
        if off <= r0 < off + n:
            return 2 * off + rank * n + (r0 - off)
    raise ValueError(r0)


def gb_from_xb(xb0, xb1, TL):
    parts = []
    for (off, n) in xb_pieces(TL):
        parts += [xb0[off:off + n], xb1[off:off + n]]
    return np.ascontiguousarray(np.concatenate(parts, axis=0))


def build(mode, TL=4096, groups=None):
    P = Prog(mode, TL)
    nc = P.nc
    TT = P.TT
    NQT = TL // 512
    fused = mode == "FUSED"
    phases = PHASES[mode]
    t_phases = [p for p in phases if p[0] == "T"]
    a_layers = [p[1] for p in phases if p[0] == "A"]
    mod_layers = sorted({l for p in t_phases for l in (p[1], p[2]) if l is not None})
    proj_layers = [p[2] for p in t_phases if p[2] is not None]
    prev_layers = [p[1] for p in t_phases if p[1] is not None]
    has_final = any(p[3] for p in t_phases)
    XL, NR = xb_layout(TL)
    es = ExitStack()
    with es:
        K = KB(nc, es)

        def sb(name, shape, dt):
            return es.enter_context(nc.sbuf_tensor(name, list(shape), dt, align_bytes=256))

        ps = [es.enter_context(nc.psum_tensor("ps%d" % i, [128, 512], F32)) for i in range(8)]
        Bps = [Buf("ps%d" % i) for i in range(8)]

        ones_bf = sb("ones_bf", [128, 128], BF16)
        B_ones = Buf("ones")
        K.op("dve", lambda e: e.memset(ones_bf[:], 1.0), [], [B_ones])
        blk_bf = sb("blk_bf", [128, 128], BF16)
        B_blk = Buf("blk")
        K.op("dve", lambda e: e.memset(blk_bf[:], 0.0), [], [B_blk])
        K.op("dve", lambda e: e.memset(blk_bf[0:64, 0:64], 1.0), [], [B_blk])
        K.op("dve", lambda e: e.memset(blk_bf[64:128, 64:128], 1.0), [], [B_blk])

        cc_d = P.din("cc", [128, 8, 2])
        W = {}
        for l in mod_layers:
            W["wada%d" % l] = P.din("wada%d" % l, [D, 9 * D])
            W["bT%d" % l] = P.din("bT%d" % l, [128, 72])
            W["nrm%d" % l] = P.din("nrm%d" % l, [128, 3, 8])
        for l in prev_layers:
            W["wout%d" % l] = P.din("wout%d" % l, [D, D])
            W["wg2_%d" % l] = P.din("wg2_%d" % l, [D, FF])
            W["wu2_%d" % l] = P.din("wu2_%d" % l, [D, FF])
            W["wd2_%d" % l] = P.din("wd2_%d" % l, [FF, D])
        for l in proj_layers:
            W["wg1_%d" % l] = P.din("wg1_%d" % l, [D, FF])
            W["wu1_%d" % l] = P.din("wu1_%d" % l, [D, FF])
            W["wd1_%d" % l] = P.din("wd1_%d" % l, [FF, D])
            W["wqk%d" % l] = P.din("wqk%d" % l, [D, 20 * 128])
            W["wv%d" % l] = P.din("wv%d" % l, [D, 640])
            W["qkn%d" % l] = P.din("qkn%d" % l, [128, 4])
        if proj_layers:
            W["cosT"] = P.din("cosT", [128, TL])
            W["sinT"] = P.din("sinT", [128, TL])
        if has_final:
            W["nfin"] = P.din("nfin", [128, 8])
        if a_layers:
            Mw_d = P.din("Mw", [128, 8, 512], BF16)
            zmask_d = P.din("zmask", [128, 2, 1408])
            mrow_d = P.din("mrow", [128, 16, 512], BF16)
            for l in a_layers:
                W["sinkT%d" % l] = P.din("sinkT%d" % l, [128, 6])
                W["zraw%d" % l] = P.din("zraw%d" % l, [6, 128, 1408])

        S = {}
        hbuf = {}
        hT_i = P.din("hT_i", [D, TT])
        B_hin = Buf("hT_i", dram=True)
        if mode == "L1":
            l = 0
            S[0] = dict(qT=P.dtmp("qT0", [D, TT], BF16), kT=P.dtmp("kT0", [640, TT], BF16),
                        v=P.dtmp("v0", [TT, 640], BF16), XB=P.dtmp("XB0", [NR, 1024], BF16),
                        GB=nc.dram_tensor("GB0", [2 * NR, 1024], BF16, kind="Internal", addr_space="Local").ap(),
                        B_qkv=Buf("qkv0", dram=True), B_XB=Buf("XB0", dram=True), B_GB=Buf("GB0", dram=True))
            S[1] = dict(qT=P.dout("qT_o", [D, TT], BF16), kT=P.dout("kT_o", [640, TT], BF16),
                        v=P.dout("v_o", [TT, 640], BF16), XB=P.dout("XB", [NR, 1024], BF16),
                        B_qkv=Buf("qkv_o", dram=True), B_XB=Buf("XB", dram=True))
            hA = P.dtmp("hA", [D, TT]); B_hA = Buf("hA", dram=True)
            hT_o = P.dout("hT_o", [D, TT])
            h_io = [(hT_i, B_hin, hA, B_hA), (hA, B_hA, hT_o, Buf("hT_o", dram=True))]
        elif fused:
            for l in (0, 1):
                S[l] = dict(qT=P.dtmp("qT%d" % l, [D, TT], BF16), kT=P.dtmp("kT%d" % l, [640, TT], BF16),
                            v=P.dtmp("v%d" % l, [TT, 640], BF16), XB=P.dtmp("XB%d" % l, [NR, 1024], BF16),
                            GB=nc.dram_tensor("GB%d" % l, [2 * NR, 1024], BF16, kind="Internal",
                                              addr_space="Local").ap(),
                            B_qkv=Buf("qkv%d" % l, dram=True), B_XB=Buf("XB%d" % l, dram=True), B_GB=Buf("GB%d" % l, dram=True))
            hA = P.dtmp("hA", [D, TT]); hB = P.dtmp("hB", [D, TT])
            B_hA = Buf("hA", dram=True); B_hB = Buf("hB", dram=True)
            outT = P.dout("outT", [D, TL])
            h_io = [(hT_i, B_hin, hA, B_hA), (hA, B_hA, hB, B_hB), (hB, B_hB, None, None)]
        else:
            h_io = []
            if a_layers:
                l = a_layers[0]
                S[l] = dict(qT=P.din("qT_i", [D, TT], BF16), kT=P.din("kT_i", [640, TT], BF16),
                            v=P.din("v_i", [TT, 640], BF16), GB=P.din("GB", [2 * NR, 1024], BF16),
                            B_qkv=Buf("qkv_i", dram=True), B_GB=Buf("GB", dram=True))
            if has_final:
                outT = P.dout("outT", [D, TL])
                h_io.append((hT_i, B_hin, None, None))
            else:
                l = proj_layers[0]
                hT_o = P.dout("hT_o", [D, TT])
                S[l] = dict(qT=P.dout("qT_o", [D, TT], BF16), kT=P.dout("kT_o", [640, TT], BF16),
                            v=P.dout("v_o", [TT, 640], BF16),
                            XB=P.dout("XB", [NR, 1024], BF16),
                            B_qkv=Buf("qkv_o", dram=True), B_XB=Buf("XB", dram=True))
                h_io.append((hT_i, B_hin, hT_o, Buf("hT_o", dram=True)))
        if a_layers:
            yT = P.dtmp("yT", [D, TT], BF16)
            B_yT = Buf("yT", dram=True)
        B_out = Buf("outs", dram=True)

        modT = {}; modv = {}; B_mod = {}; nrm_sb = {}; bT_sb = {}
        for l in mod_layers:
            modT[l] = sb("modT%d" % l, [128, 72, 2], F32)
            modv[l] = sb("modv%d" % l, [128, 3, 3, 2, 8], F32)
            nrm_sb[l] = sb("nrm_sb%d" % l, [128, 3, 8], F32)
            bT_sb[l] = sb("bT_sb%d" % l, [128, 72], F32)
            B_mod[l] = Buf("mod%d" % l)
        cc_sb = sb("cc_sb", [128, 8, 2], F32)
        sc_sb = sb("sc_sb", [128, 8, 2], BF16)
        B_cc = Buf("cc")
        mods_done = set()

        def pack_xb(l):
            return

        def exchange(l):
            s_ = S[l]
            K._deps("pool", [s_["B_XB"]], [s_["B_GB"]])
            for pi_, (off, n) in enumerate(xb_pieces(TL)):
                key = "cc%d_%d" % (l, pi_)
                K.dsem[key] = es.enter_context(nc.semaphore("d_" + key))
                K.dcnt[key] = 1
                nc.gpsimd.collective_compute("AllGather", ALU.bypass, replica_groups=groups,
                                             ins=[s_["XB"][off:off + n, :]],
                                             outs=[s_["GB"][2 * off:2 * off + 2 * n, :]]).then_inc(K.dsem[key])
                K._mark(key, 1, [s_["B_XB"]], [s_["B_GB"]])

        def attention_phase(l):
            with_ctxq = l == 0
            s_ = S[l]
            qT_i, kT_own, v_own, GB = s_["qT"], s_["kT"], s_["v"], s_["GB"]
            B_own, B_GB = s_["B_qkv"], s_["B_GB"]
            sink_d = W["sinkT%d" % l]
            zraw_d = W["zraw%d" % l]

            def gbv(rank, nm, shape=None):
                r0, shp = XL[nm]
                return xb_view(GB, gb_row(TL, rank, r0), shape or shp)

            aes = ExitStack()
            with aes:
                def asb(name, shape, dt):
                    return aes.enter_context(nc.sbuf_tensor(name + "_%d" % l, list(shape), dt, align_bytes=256))

                NKG = (2 * TL + TC)
                kT_sb = [asb("kT_sb%d" % i, [64, NKG], BF16) for i in range(2)]
                va_sb = [asb("va_sb%d" % i, [128, NKG // 128, 128], BF16) for i in range(2)]
                B_kT = [Buf("kT%d" % i) for i in range(2)]
                B_va = [Buf("va%d" % i) for i in range(2)]
                q_sb = [asb("q_sb%d" % i, [64, 512], BF16) for i in range(2)]
                B_q = [Buf("q%d" % i) for i in range(2)]
                NPT = 6
                pT = [asb("pT%d" % i, [128, 512], BF16) for i in range(NPT)]
                B_pT = [Buf("pT%d" % i) for i in range(NPT)]
                pf = [asb("pf%d" % i, [128, 512], F32) for i in range(4)]
                B_pf = [Buf("pf%d" % i) for i in range(4)]
                y_st = [asb("y_st%d" % i, [64, 512], BF16) for i in range(2)]
                B_yst = [Buf("yst%d" % i) for i in range(2)]
                lnd = asb("lnd", [64, 512], F32)
                B_lnd = Buf("lnd")
                rden = asb("rden", [64, 512], F32)
                B_rden = Buf("rden")
                Mw_sb = asb("Mw_sb", [128, 8, 512], BF16)
                B_Mw = Buf("Mw")
                mrow_sb = asb("mrow_sb", [128, 16, 512], BF16)
                B_mrow = Buf("mrow")
                zmask_sb = asb("zmask_sb", [128, 2, 1408], F32)
                B_zmask = Buf("zmask")
                zraw_sb = asb("zraw_sb", [128, 1408], F32)
                B_zraw = Buf("zraw")
                zexp_sb = asb("zexp_sb", [128, 1408], F32)
                B_zexp = Buf("zexp")
                zfull = [asb("zfull%d" % i, [128, 1408], F32) for i in range(2)]
                zint = [asb("zint%d" % i, [128, 1408], F32) for i in range(2)]
                B_z = [Buf("z%d" % i) for i in range(2)]
                sink_sb = asb("sink_sb", [128, 6], F32)
                es_sb = asb("es_sb", [128, 6], F32)
                B_es = Buf("es")

                K.dma("sp", Mw_sb[:], Mw_d, [], [B_Mw])
                K.dma("sp", mrow_sb[:], mrow_d, [], [B_mrow])
                K.dma("sp", zmask_sb[:], zmask_d, [], [B_zmask])
                K.dma("sp", sink_sb[:], sink_d, [], [B_es])
                K.op("act", lambda e: e.activation(out=es_sb[:], in_=sink_sb[:], func=AF.Exp), [B_es], [B_es])
                for i in range(2):
                    K.op("dve", lambda e, i=i: e.memset(va_sb[i][:, :, 64:128], 1.0), [], [B_va[i]])

                NTL = TL // 128
                groups_ = []
                for g in range(2):
                    r = slice(g * 64, (g + 1) * 64)
                    kp = [(0, 128, gbv(0, "kw_last")[r, :], B_GB), (128, TL, kT_own[r, 0:TL], B_own),
                          (128 + TL, 128, gbv(1, "kw_first")[r, :], B_GB), (256 + TL, TC, kT_own[r, TL:TT], B_own)]
                    vp = [(0, 1, gbv(0, "vw_last")[:, r], B_GB), (1, NTL, v_own[0:TL, r], B_own),
                          (1 + NTL, 1, gbv(1, "vw_first")[:, r], B_GB), (2 + NTL, 2, v_own[TL:TT, r], B_own)]
                    groups_.append(("win", kp, vp, [((g * 3 + i) * 64, g * 3 + i) for i in range(3)], None))
                for g in range(2):
                    r = slice(g * 64, (g + 1) * 64)
                    r2 = slice(128 + g * 64, 128 + (g + 1) * 64)
                    kp = [(0, TL, gbv(0, "kg")[r, :], B_GB), (TL, TL, gbv(1, "kg")[r, :], B_GB),
                          (2 * TL, TC, kT_own[r2, TL:TT], B_own)]
                    vp = [(0, NTL, gbv(0, "vg")[:, r], B_GB), (NTL, NTL, gbv(1, "vg")[:, r], B_GB),
                          (2 * NTL, 2, v_own[TL:TT, r2], B_own)]
                    groups_.append(("glob", kp, vp, [(384 + (g * 2 + i) * 64, None) for i in range(2)], None))
                for h in range(6):
                    r = slice(h * 64, (h + 1) * 64)
                    r2 = slice(256 + h * 64, 256 + (h + 1) * 64)
                    j, rr = h // 2, slice((h % 2) * 64, (h % 2) * 64 + 64)
                    r0f, _ = XL["vn_first"]
                    r0l, _ = XL["vn_last"]
                    vnf = xb_view(GB, gb_row(TL, 1, r0f + j * 32), (256, 128))[:, rr]
                    vnl = xb_view(GB, gb_row(TL, 0, r0l + j * 32), (256, 128))[:, rr]
                    kp = [(0, 256, gbv(0, "kn_last")[r, :], B_GB), (256, TL, kT_own[r2, 0:TL], B_own),
                          (256 + TL, 256, gbv(1, "kn_first")[r, :], B_GB), (512 + TL, TC, kT_own[r2, TL:TT], B_own)]
                    vp = [(0, 2, vnl, B_GB), (2, NTL, v_own[0:TL, r2], B_own),
                          (2 + NTL, 2, vnf, B_GB), (4 + NTL, 2, v_own[TL:TT, r2], B_own)]
                    groups_.append(("nbr", kp, vp, [(640 + h * 64, None)], h))

                def load_group(gi):
                    kind, kp, vp, heads, zh = groups_[gi]
                    s = gi % 2
                    for (c0, ncol, src, bsrc) in kp:
                        K.dma("sp", kT_sb[s][:, c0:c0 + ncol], src, [bsrc], [B_kT[s]])
                    for (t0, ntile, src, bsrc) in vp:
                        K.dma("sp", va_sb[s][:, t0:t0 + ntile, 0:64],
                              src.rearrange("(t p) d -> p t d", p=128), [bsrc], [B_va[s]])

                state = {"qs": 0, "pt": 0, "pf": 0, "pss": 0, "pso": 0, "ys": 0, "zs": 0}

                def load_q(q_row, q_col, nq):
                    qs = state["qs"]; state["qs"] ^= 1
                    K.dma("sp", q_sb[qs][:, 0:nq], qT_i[q_row:q_row + 64, q_col:q_col + nq],
                          [B_own], [B_q[qs]])
                    return qs

                def attend(s, qs, q_row, q_col, nq, ktiles, sink_idx):
                    po = 3 + state["pso"]; state["pso"] ^= 1
                    n = len(ktiles)
                    ptile = {}

                    def emit_qk(i):
                        kcol, vt, masks = ktiles[i]
                        b = (0, 1, 2, 5, 6)[state["pss"]]; state["pss"] = (state["pss"] + 1) % 5
                        K.op("pe", lambda e: e.matmul(ps[b][:, 0:nq], lhsT=kT_sb[s][0:64, kcol:kcol + 128],
                                                      rhs=q_sb[qs][0:64, 0:nq], start=True, stop=True),
                             [B_kT[s], B_q[qs]], [Bps[b]])
                        pt = state["pt"]; state["pt"] = (state["pt"] + 1) % NPT
                        ptile[i] = pt
                        if not masks:
                            K.op("act", lambda e: e.activation(out=pT[pt][:, 0:nq], in_=ps[b][:, 0:nq],
                                                               func=AF.Exp, scale=0.125),
                                 [Bps[b]], [B_pT[pt]])
                        else:
                            f = state["pf"]; state["pf"] = (state["pf"] + 1) % 4
                            K.op("act", lambda e: e.activation(out=pf[f][:, 0:nq], in_=ps[b][:, 0:nq],
                                                               func=AF.Exp, scale=0.125),
                                 [Bps[b]], [B_pf[f]])
                            cur = f
                            for mi, (m_ap, m_buf) in enumerate(masks):
                                lastm = mi == len(masks) - 1
                                if lastm:
                                    K.op("dve", lambda e: e.tensor_tensor(out=pT[pt][:, 0:nq], in0=pf[cur][:, 0:nq],
                                                                          in1=m_ap, op=ALU.mult),
                                         [B_pf[cur], m_buf], [B_pT[pt]])
                                else:
                                    f2 = state["pf"]; state["pf"] = (state["pf"] + 1) % 4
                                    K.op("dve", lambda e: e.tensor_tensor(out=pf[f2][:, 0:nq], in0=pf[cur][:, 0:nq],
                                                                          in1=m_ap, op=ALU.mult),
                                         [B_pf[cur], m_buf], [B_pf[f2]])
                                    cur = f2

                    def emit_pv(i):
                        kcol, vt, masks = ktiles[i]
                        pt = ptile[i]
                        K.op("pe", lambda e: e.matmul(ps[po][:, 0:nq], lhsT=va_sb[s][:, vt, :],
                                                      rhs=pT[pt][:, 0:nq], start=(i == 0), stop=(i == n - 1)),
                             [B_va[s], B_pT[pt]], [Bps[po]], inc=(i == n - 1))

                    LA = 3
                    for i in range(min(LA, n)):
                        emit_qk(i)
                    for i in range(n):
                        if i + LA < n:
                            emit_qk(i + LA)
                        emit_pv(i)
                    if sink_idx is None:
                        K.op("act", lambda e: e.activation(out=lnd[:, 0:nq], in_=ps[po][64:128, 0:nq], func=AF.Ln),
                             [Bps[po]], [B_lnd])
                    else:
                        K.op("act", lambda e: e.activation(out=lnd[:, 0:nq], in_=ps[po][64:128, 0:nq], func=AF.Ln,
                                                           bias=es_sb[64:128, sink_idx:sink_idx + 1]),
                             [Bps[po], B_es], [B_lnd])
                    K.op("act", lambda e: e.activation(out=rden[:, 0:nq], in_=lnd[:, 0:nq], func=AF.Exp, scale=-1.0),
                         [B_lnd], [B_rden])
                    ys = state["ys"]; state["ys"] ^= 1
                    K.op("dve", lambda e: e.tensor_tensor(out=y_st[ys][:, 0:nq], in0=ps[po][0:64, 0:nq],
                                                          in1=rden[:, 0:nq], op=ALU.mult),
                         [Bps[po], B_rden], [B_yst[ys]])
                    K.dma("sp", yT[q_row:q_row + 64, q_col:q_col + nq], y_st[ys][:, 0:nq],
                          [B_yst[ys]], [B_yT])

                load_group(0)
                for gi, (kind, kp, vp, heads, zh) in enumerate(groups_):
                    s = gi % 2
                    if gi + 1 < len(groups_):
                        load_group(gi + 1)
                    if kind == "nbr":
                        zs = state["zs"]; state["zs"] ^= 1
                        K.dma("sp", zraw_sb[:], zraw_d[zh], [], [B_zraw])
                        K.op("act", lambda e: e.activation(out=zexp_sb[:], in_=zraw_sb[:], func=AF.Exp),
                             [B_zraw], [B_zexp])
                        K.op("dve", lambda e: e.tensor_tensor(out=zfull[zs][:], in0=zexp_sb[:], in1=zmask_sb[:, 0, :],
                                                              op=ALU.mult), [B_zexp, B_zmask], [B_z[zs]])
                        K.op("dve", lambda e: e.tensor_tensor(out=zint[zs][:], in0=zexp_sb[:], in1=zmask_sb[:, 1, :],
                                                              op=ALU.mult), [B_zexp, B_zmask], [B_z[zs]])
                    calls = []
                    for (q_row, sink_idx) in heads:
                        for t in range(NQT):
                            kt = []
                            if kind == "win":
                                cb = TL + 256
                                for m in range(6):
                                    blk = 4 * t + m - 1
                                    mi = m
                                    if t == 0 and m == 0:
                                        mi = 6
                                    if t == NQT - 1 and m == 5:
                                        mi = 7
                                    kt.append((128 * (blk + 1), blk + 1, [(Mw_sb[:, mi, :], B_Mw)]))
                            elif kind == "glob":
                                cb = 2 * TL
                                for i in range(2 * TL // 128):
                                    kt.append((128 * i, i, []))
                            else:
                                cb = TL + 512
                                for j in range(8):
                                    kcol = 64 * (8 * t + 2 * j)
                                    lo = (14 - 2 * j) * 64
                                    if t == 0 or t == NQT - 1:
                                        e_idx = 0 if t == 0 else 1
                                        masks = [(zfull[zs][:, lo:lo + 512], B_z[zs]),
                                                 (mrow_sb[:, e_idx * 8 + j, :], B_mrow)]
                                    else:
                                        masks = [(zint[zs][:, lo:lo + 512], B_z[zs])]
                                    kt.append((kcol, kcol // 128, masks))
                            for i in range(2):
                                kt.append((cb + 128 * i, cb // 128 + i, []))
                            calls.append((q_row, t * 512, 512, kt, sink_idx))
                        if with_ctxq:
                            cb = {"win": TL + 256, "glob": 2 * TL, "nbr": TL + 512}[kind]
                            kt = [(cb + 128 * i, cb // 128 + i, []) for i in range(2)]
                            calls.append((q_row, TL, TC, kt, sink_idx))
                    slots = {0: load_q(*calls[0][:3])}
                    for ci, cl in enumerate(calls):
                        if ci + 1 < len(calls):
                            slots[ci + 1] = load_q(*calls[ci + 1][:3])
                        attend(s, slots[ci], *cl)

        def token_phase(pi, l_prev, l_next, final, hT_in, B_hTi, hT_o, B_hTo):
            tes = ExitStack()
            with tes:
                def tsb(name, shape, dt):
                    return tes.enter_context(nc.sbuf_tensor(name + "_t%d" % pi, list(shape), dt, align_bytes=256))

                h_sb = tsb("h_sb", [128, 8, NT], F32); B_h = Buf("h")
                u_sb = tsb("u_sb", [128, 8, NT], BF16); B_u = Buf("u")
                hid_sb = tsb("hid_sb", [128, NF, NT], BF16); B_hid = Buf("hid")
                wA = [(tsb("wA%d" % i, [128, 8, 512], BF16), Buf("wA%d" % i)) for i in range(4)]
                wDt = [(tsb("wD", [128, NF, D], BF16), Buf("wD"))]
                tmpf = [tsb("tmpf%d" % i, [128, 512], F32) for i in range(4)]
                B_tmpf = [Buf("tmpf%d" % i) for i in range(4)]
                rstd_sb = tsb("rstd_sb", [128, 512], F32); B_rstd = Buf("rstd")
                sq_sb = [tsb("sq%d" % i, [128, 512], BF16) for i in range(2)]
                B_sq = [Buf("sq%d" % i) for i in range(2)]
                B_cs = Buf("cs")
                B_stg = [Buf("stg%d" % i) for i in range(2)]
                B_vst = [Buf("vst%d" % i) for i in range(2)]
                if l_next is not None:
                    cos_sb = tsb("cos_sb", [128, 512], F32)
                    sin_sb = tsb("sin_sb", [128, 512], F32)
                    stg = [tsb("stg%d" % i, [128, 512], BF16) for i in range(2)]
                    vst = [tsb("vst%d" % i, [128, 640], BF16) for i in range(2)]
                ostg = [tsb("ostg%d" % i, [128, 512], F32) for i in range(2)] if final else None
                B_ostg = [Buf("ostg%d" % i) for i in range(2)]
                if l_next is not None:
                    qkn_sb = tsb("qkn_sb", [128, 4], F32)
                if final:
                    nfin_sb = tsb("nfin_sb", [128, 8], F32)
                B_small = Buf("small")
                if l_next is not None:
                    s_o = S[l_next]
                    qT_o, kT_o, v_o, B_qkvo = s_o["qT"], s_o["kT"], s_o["v"], s_o["B_qkv"]
                    XBo, B_XBo = s_o.get("XB"), s_o.get("B_XB")

                    def xbv(nm):
                        r0x, shp = XL[nm]
                        return xb_view(XBo, r0x, shp)

                ws = WStream(K, {"A": wA, "D": wDt}, "p%d" % pi)

                tiles = [(o, w, 0) for (o, w) in _blocks(TL, NT)]
                if not final:
                    tiles.append((TL, TC, 1))

                def a_load(src2d, c0, cw):
                    src = src2d.rearrange("(c p) n -> p c n", p=128)[:, :, c0:c0 + cw]
                    return [(lambda t, cw=cw: t[:, :, 0:cw], src)]

                def d_load(src2d):
                    src = src2d.rearrange("(j p) n -> p j n", p=128)
                    out = []
                    for j0 in range(0, NF, 4):
                        j1 = min(NF, j0 + 4)
                        out.append((lambda t, j0=j0, j1=j1: t[:, j0:j1, :], src[:, j0:j1, :]))
                    return out

                new_mods = [l for l in mod_layers if l not in mods_done]
                for l in new_mods:
                    for (c0, cw) in _blocks(9 * D):
                        ws.add("A", a_load(W["wada%d" % l], c0, cw))
                for (o, w, v) in tiles:
                    if l_prev is not None:
                        l = l_prev
                        for (c0, cw) in _blocks(D):
                            ws.add("A", a_load(W["wout%d" % l], c0, cw))
                        for (c0, cw) in _blocks(FF):
                            ws.add("A", a_load(W["wg2_%d" % l], c0, cw))
                            ws.add("A", a_load(W["wu2_%d" % l], c0, cw))
                        ws.add("D", d_load(W["wd2_%d" % l]))
                    if l_next is not None:
                        l = l_next
                        for (c0, cw) in _blocks(FF):
                            ws.add("A", a_load(W["wg1_%d" % l], c0, cw))
                            ws.add("A", a_load(W["wu1_%d" % l], c0, cw))
                        ws.add("D", d_load(W["wd1_%d" % l]))
                        for (c0, cw) in _blocks(20 * 128):
                            ws.add("A", a_load(W["wqk%d" % l], c0, cw))
                        for (c0, cw) in _blocks(640):
                            ws.add("A", a_load(W["wv%d" % l], c0, cw))
                ws.start()

                if new_mods:
                    K.dma("sp", cc_sb[:], cc_d, [], [B_cc])
                    K.op("act", lambda e: e.activation(out=sc_sb[:], in_=cc_sb[:], func=AF.Silu), [B_cc], [B_cc])
                for l in new_mods:
                    K.dma("sp", nrm_sb[l][:], W["nrm%d" % l], [], [B_mod[l]])
                    K.dma("sp", bT_sb[l][:], W["bT%d" % l], [], [B_mod[l]])
                if l_next is not None:
                    K.dma("sp", qkn_sb[:], W["qkn%d" % l_next], [], [B_small])
                if final:
                    K.dma("sp", nfin_sb[:], W["nfin"], [], [B_small])

                for l in new_mods:
                    mods_done.add(l)
                    for bi, (c0, cw) in enumerate(_blocks(9 * D)):
                        wt, wb = ws.take("A")
                        pb = 6 + (bi % 2)
                        for jj in range(4):
                            for c in range(8):
                                K.op("pe", lambda e: e.matmul(
                                    ps[pb][:, 2 * jj:2 * jj + 2], lhsT=wt[:, c, jj * 128:(jj + 1) * 128],
                                    rhs=sc_sb[:, c, :], start=(c == 0), stop=(c == 7)),
                                    [wb, B_cc], [Bps[pb]], inc=(c == 7))
                        ws.release("A")
                        j0 = bi * 4
                        for v in range(2):
                            K.op("dve", lambda e: e.tensor_tensor(
                                out=modT[l][:, j0:j0 + 4, v],
                                in0=ps[pb][:, 0:8].rearrange("p (j v) -> p j v", v=2)[:, :, v],
                                in1=bT_sb[l][:, j0:j0 + 4], op=ALU.add),
                                [Bps[pb], B_mod[l]], [B_mod[l]])
                    for s_ in range(3):
                        for v in range(2):
                            K.op("dve", lambda e: e.scalar_tensor_tensor(
                                out=modv[l][:, s_, 0, v, :], in0=modT[l][:, (3 * s_ + 1) * 8:(3 * s_ + 1) * 8 + 8, v],
                                scalar=1.0, in1=nrm_sb[l][:, s_, :], op0=ALU.add, op1=ALU.mult),
                                [B_mod[l]], [B_mod[l]])
                            K.op("dve", lambda e: e.tensor_copy(
                                out=modv[l][:, s_, 1, v, :], in_=modT[l][:, (3 * s_) * 8:(3 * s_) * 8 + 8, v]),
                                [B_mod[l]], [B_mod[l]])
                            gsc = 1.0 if s_ == 1 else 0.5
                            K.op("dve", lambda e: e.tensor_scalar(
                                out=modv[l][:, s_, 2, v, :], in0=modT[l][:, (3 * s_ + 2) * 8:(3 * s_ + 2) * 8 + 8, v],
                                scalar1=gsc, scalar2=None, op0=ALU.mult),
                                [B_mod[l]], [B_mod[l]])

                def mv(l, s_, k, v, c):
                    return modv[l][:, s_, k, v, c:c + 1]

                st = {"tf": 0, "sq": 0, "pg": 0, "pu": 0, "pd": 0, "stg": 0, "vst": 0, "ostg": 0}

                def rstd_for(src_fn, nchunk, w, lhs_ones, b_ones, inv_n, psb):
                    for c in range(nchunk):
                        q = st["sq"]; st["sq"] ^= 1
                        ap, bf = src_fn(c)
                        K.op("act", lambda e: e.activation(out=sq_sb[q][:, 0:w], in_=ap, func=AF.Square),
                             [bf], [B_sq[q]])
                        K.op("pe", lambda e: e.matmul(ps[psb][:, 0:w], lhsT=lhs_ones, rhs=sq_sb[q][:, 0:w],
                                                      start=(c == 0), stop=(c == nchunk - 1)),
                             [B_sq[q], b_ones], [Bps[psb]], inc=True)
                    K.op("act", lambda e: e.activation(out=rstd_sb[:, 0:w], in_=ps[psb][:, 0:w], func=AF.Ln,
                                                       scale=inv_n, bias=EPS),
                         [Bps[psb]], [B_rstd])
                    K.op("act", lambda e: e.activation(out=rstd_sb[:, 0:w], in_=rstd_sb[:, 0:w], func=AF.Exp,
                                                       scale=-0.5),
                         [B_rstd], [B_rstd])

                def norm_mod(l, s_, v, ntok):
                    for (o, w) in _blocks(ntok):
                        rstd_for(lambda c: (h_sb[:, c, o:o + w], B_h), 8, w, ones_bf[:], B_ones, 1.0 / D, 4)
                        for c in range(8):
                            f = st["tf"]; st["tf"] = (st["tf"] + 1) % 4
                            K.op("dve", lambda e: e.tensor_tensor(out=tmpf[f][:, 0:w], in0=h_sb[:, c, o:o + w],
                                                                  in1=rstd_sb[:, 0:w], op=ALU.mult),
                                 [B_h, B_rstd], [B_tmpf[f]])
                            K.op("act", lambda e: e.activation(out=u_sb[:, c, o:o + w], in_=tmpf[f][:, 0:w],
                                                               func=AF.Identity, scale=mv(l, s_, 0, v, c),
                                                               bias=mv(l, s_, 1, v, c)),
                                 [B_tmpf[f], B_mod[l]], [B_u])

                def ffn(l, s_, v, ntok):
                    norm_mod(l, s_, v, ntok)
                    subs = _blocks(ntok)
                    for (c0, cw) in _blocks(FF):
                        gt, gb = ws.take("A")
                        ut, ub = ws.take("A")
                        for jj in range(cw // 128):
                            j = c0 // 128 + jj
                            for (o, w) in subs:
                                pg = st["pg"]; st["pg"] ^= 1
                                pu = 2 + st["pu"]; st["pu"] ^= 1
                                for c in range(8):
                                    K.op("pe", lambda e: e.matmul(ps[pg][:, 0:w], lhsT=gt[:, c, jj * 128:(jj + 1) * 128],
                                                                  rhs=u_sb[:, c, o:o + w], start=(c == 0), stop=(c == 7)),
                                         [gb, B_u], [Bps[pg]], inc=(c == 7))
                                for c in range(8):
                                    K.op("pe", lambda e: e.matmul(ps[pu][:, 0:w], lhsT=ut[:, c, jj * 128:(jj + 1) * 128],
                                                                  rhs=u_sb[:, c, o:o + w], start=(c == 0), stop=(c == 7)),
                                         [ub, B_u], [Bps[pu]], inc=(c == 7))
                                f = st["tf"]; st["tf"] = (st["tf"] + 1) % 4
                                K.op("act", lambda e: e.activation(out=tmpf[f][:, 0:w], in_=ps[pg][:, 0:w], func=AF.Silu),
                                     [Bps[pg]], [B_tmpf[f]])
                                K.op("dve", lambda e: e.tensor_tensor(out=hid_sb[:, j, o:o + w], in0=tmpf[f][:, 0:w],
                                                                      in1=ps[pu][:, 0:w], op=ALU.mult),
                                     [B_tmpf[f], Bps[pu]], [B_hid])
                        ws.release("A")
                        ws.release("A")
                    dt_, db = ws.take("D")
                    for d in range(8):
                        for (o, w) in subs:
                            pd = 4 + st["pd"]; st["pd"] ^= 1
                            for j in range(NF):
                                K.op("pe", lambda e: e.matmul(ps[pd][:, 0:w], lhsT=dt_[:, j, d * 128:(d + 1) * 128],
                                                              rhs=hid_sb[:, j, o:o + w], start=(j == 0), stop=(j == NF - 1)),
                                     [db, B_hid], [Bps[pd]], inc=(j == NF - 1))
                            K.op("dve", lambda e: e.scalar_tensor_tensor(
                                out=h_sb[:, d, o:o + w], in0=ps[pd][:, 0:w], scalar=mv(l, s_, 2, v, d),
                                in1=h_sb[:, d, o:o + w], op0=ALU.mult, op1=ALU.add),
                                [Bps[pd], B_mod[l], B_h], [B_h])
                    ws.release("D")

                def wout_step(l, v, tok0, ntok):
                    K.dma("sp", hid_sb[:, 0:8, 0:ntok],
                          yT.rearrange("(c p) n -> p c n", p=128)[:, :, tok0:tok0 + ntok], [B_yT], [B_hid])
                    subs = _blocks(ntok)
                    for (c0, cw) in _blocks(D):
                        wt, wb = ws.take("A")
                        for dd in range(4):
                            d = c0 // 128 + dd
                            for (o, w) in subs:
                                pd = 4 + st["pd"]; st["pd"] ^= 1
                                for c in range(8):
                                    K.op("pe", lambda e: e.matmul(ps[pd][:, 0:w], lhsT=wt[:, c, dd * 128:(dd + 1) * 128],
                                                                  rhs=hid_sb[:, c, o:o + w], start=(c == 0), stop=(c == 7)),
                                         [wb, B_hid], [Bps[pd]], inc=(c == 7))
                                K.op("dve", lambda e: e.scalar_tensor_tensor(
                                    out=h_sb[:, d, o:o + w], in0=ps[pd][:, 0:w], scalar=mv(l, 1, 2, v, d),
                                    in1=h_sb[:, d, o:o + w], op0=ALU.mult, op1=ALU.add),
                                    [Bps[pd], B_mod[l], B_h], [B_h])
                        ws.release("A")

                def proj_step(l, v, tok0, ntok):
                    norm_mod(l, 1, v, ntok)
                    subs = _blocks(ntok)
                    latent = v == 0
                    chunk_plan = []
                    cidx = 0
                    for (kind, dest, row) in QK_ITEMS:
                        if kind == "plain":
                            chunk_plan.append((kind, dest, row, cidx, None)); cidx += 1
                        else:
                            chunk_plan.append((kind, dest, row, cidx, cidx + 1)); cidx += 2
                    assert cidx == 20
                    cur_blk = -1
                    wt = wb = None
                    for (kind, dest, row, cx, cs_) in chunk_plan:
                        blk = cx // 4
                        if blk != cur_blk:
                            if cur_blk >= 0:
                                ws.release("A")
                            wt, wb = ws.take("A")
                            cur_blk = blk
                        lx = (cx % 4) * 128
                        for (o, w) in subs:
                            px = 0 + st["pg"]; st["pg"] ^= 1
                            for c in range(8):
                                K.op("pe", lambda e: e.matmul(ps[px][:, 0:w], lhsT=wt[:, c, lx:lx + 128],
                                                              rhs=u_sb[:, c, o:o + w], start=(c == 0), stop=(c == 7)),
                                     [wb, B_u], [Bps[px]], inc=(c == 7))
                            if cs_ is not None and latent:
                                ls = (cs_ % 4) * 128
                                pxs = 2 + st["pu"]; st["pu"] ^= 1
                                for c in range(8):
                                    K.op("pe", lambda e: e.matmul(ps[pxs][:, 0:w], lhsT=wt[:, c, ls:ls + 128],
                                                                  rhs=u_sb[:, c, o:o + w], start=(c == 0), stop=(c == 7)),
                                         [wb, B_u], [Bps[pxs]], inc=(c == 7))
                            sg = st["stg"]; st["stg"] ^= 1
                            if kind != "plain" and latent:
                                K.dma("sp", cos_sb[:, 0:w], W["cosT"][:, tok0 + o:tok0 + o + w], [], [B_cs])
                                K.dma("sp", sin_sb[:, 0:w], W["sinT"][:, tok0 + o:tok0 + o + w], [], [B_cs])
                            if kind == "nrope":
                                rstd_for(lambda c: (ps[px][:, 0:w], Bps[px]), 1, w, blk_bf[:], B_blk, 1.0 / 64, 6)
                                gcol = 0 if dest == "q" else 2
                            if kind == "plain" or (kind == "rope" and not latent):
                                K.op("act", lambda e: e.activation(out=stg[sg][:, 0:w], in_=ps[px][:, 0:w], func=AF.Copy),
                                     [Bps[px]], [B_stg[sg]])
                            elif kind == "rope":
                                f1 = st["tf"]; st["tf"] = (st["tf"] + 1) % 4
                                f2 = st["tf"]; st["tf"] = (st["tf"] + 1) % 4
                                K.op("dve", lambda e: e.tensor_tensor(out=tmpf[f1][:, 0:w], in0=ps[px][:, 0:w],
                                                                      in1=cos_sb[:, 0:w], op=ALU.mult),
                                     [Bps[px], B_cs], [B_tmpf[f1]])
                                K.op("dve", lambda e: e.tensor_tensor(out=tmpf[f2][:, 0:w], in0=ps[pxs][:, 0:w],
                                                                      in1=sin_sb[:, 0:w], op=ALU.mult),
                                     [Bps[pxs], B_cs], [B_tmpf[f2]])
                                K.op("dve", lambda e: e.tensor_tensor(out=stg[sg][:, 0:w], in0=tmpf[f1][:, 0:w],
                                                                      in1=tmpf[f2][:, 0:w], op=ALU.add),
                                     [B_tmpf[f1], B_tmpf[f2]], [B_stg[sg]])
                            elif kind == "nrope" and latent:
                                f1 = st["tf"]; st["tf"] = (st["tf"] + 1) % 4
                                f2 = st["tf"]; st["tf"] = (st["tf"] + 1) % 4
                                K.op("dve", lambda e: e.scalar_tensor_tensor(
                                    out=tmpf[f1][:, 0:w], in0=ps[px][:, 0:w], scalar=qkn_sb[:, gcol:gcol + 1],
                                    in1=cos_sb[:, 0:w], op0=ALU.mult, op1=ALU.mult),
                                    [Bps[px], B_cs, B_small], [B_tmpf[f1]])
                                K.op("dve", lambda e: e.scalar_tensor_tensor(
                                    out=tmpf[f2][:, 0:w], in0=ps[pxs][:, 0:w], scalar=qkn_sb[:, gcol + 1:gcol + 2],
                                    in1=sin_sb[:, 0:w], op0=ALU.mult, op1=ALU.mult),
                                    [Bps[pxs], B_cs, B_small], [B_tmpf[f2]])
                                K.op("dve", lambda e: e.tensor_tensor(out=tmpf[f1][:, 0:w], in0=tmpf[f1][:, 0:w],
                                                                      in1=tmpf[f2][:, 0:w], op=ALU.add),
                                     [B_tmpf[f1], B_tmpf[f2]], [B_tmpf[f1]])
                                K.op("dve", lambda e: e.tensor_tensor(out=stg[sg][:, 0:w], in0=tmpf[f1][:, 0:w],
                                                                      in1=rstd_sb[:, 0:w], op=ALU.mult),
                                     [B_tmpf[f1], B_rstd], [B_stg[sg]])
                            else:
                                K.op("dve", lambda e: e.scalar_tensor_tensor(
                                    out=stg[sg][:, 0:w], in0=ps[px][:, 0:w], scalar=qkn_sb[:, gcol:gcol + 1],
                                    in1=rstd_sb[:, 0:w], op0=ALU.mult, op1=ALU.mult),
                                    [Bps[px], B_rstd, B_small], [B_stg[sg]])
                            dst = qT_o if dest == "q" else kT_o
                            K.dma("sp", dst[row:row + 128, tok0 + o:tok0 + o + w], stg[sg][:, 0:w],
                                  [B_stg[sg]], [B_qkvo])
                            if latent and dest == "k" and XBo is not None:
                                g0 = tok0 + o
                                if row == 128:
                                    K.dma("sp", xbv("kg")[:, g0:g0 + w], stg[sg][:, 0:w],
                                          [B_stg[sg]], [B_XBo])
                                elif row == 0:
                                    if g0 == 0:
                                        K.dma("sp", xbv("kw_first"), stg[sg][:, 0:128],
                                              [B_stg[sg]], [B_XBo])
                                    if g0 + w == TL:
                                        K.dma("sp", xbv("kw_last"), stg[sg][:, w - 128:w],
                                              [B_stg[sg]], [B_XBo])
                                else:
                                    jn = (row - 256) // 128
                                    if g0 == 0:
                                        K.dma("sp", xbv("kn_first")[jn * 128:(jn + 1) * 128, :], stg[sg][:, 0:256],
                                              [B_stg[sg]], [B_XBo])
                                    if g0 + w == TL:
                                        K.dma("sp", xbv("kn_last")[jn * 128:(jn + 1) * 128, :], stg[sg][:, w - 256:w],
                                              [B_stg[sg]], [B_XBo])
                    ws.release("A")
                    v0t, v0b = ws.take("A")
                    v1t, v1b = ws.take("A")
                    for tb in range(ntok // 128):
                        pa = 0 + st["pg"]; st["pg"] ^= 1
                        pb_ = 2 + st["pu"]; st["pu"] ^= 1
                        for c in range(8):
                            K.op("pe", lambda e: e.matmul(ps[pa][:, 0:512], lhsT=u_sb[:, c, tb * 128:(tb + 1) * 128],
                                                          rhs=v0t[:, c, 0:512], start=(c == 0), stop=(c == 7)),
                                 [v0b, B_u], [Bps[pa]], inc=(c == 7))
                        for c in range(8):
                            K.op("pe", lambda e: e.matmul(ps[pb_][:, 0:128], lhsT=u_sb[:, c, tb * 128:(tb + 1) * 128],
                                                          rhs=v1t[:, c, 0:128], start=(c == 0), stop=(c == 7)),
                                 [v1b, B_u], [Bps[pb_]], inc=(c == 7))
                        vs_ = st["vst"]; st["vst"] ^= 1
                        K.op("act", lambda e: e.activation(out=vst[vs_][:, 0:512], in_=ps[pa][:, 0:512], func=AF.Copy),
                             [Bps[pa]], [B_vst[vs_]])
                        K.op("dve", lambda e: e.tensor_copy(out=vst[vs_][:, 512:640], in_=ps[pb_][:, 0:128]),
                             [Bps[pb_]], [B_vst[vs_]])
                        K.dma("sp", v_o[tok0 + tb * 128:tok0 + (tb + 1) * 128, :], vst[vs_][:],
                              [B_vst[vs_]], [B_qkvo])
                        if latent and XBo is not None:
                            t0 = tok0 + tb * 128
                            K.dma("sp", xbv("vg")[t0:t0 + 128, :], vst[vs_][:, 128:256],
                                  [B_vst[vs_]], [B_XBo])
                            if t0 == 0:
                                K.dma("sp", xbv("vw_first"), vst[vs_][:, 0:128], [B_vst[vs_]], [B_XBo])
                            if t0 == TL - 128:
                                K.dma("sp", xbv("vw_last"), vst[vs_][:, 0:128], [B_vst[vs_]], [B_XBo])
                            for (nm, base) in (("vn_first", 0), ("vn_last", TL - 256)):
                                if base <= t0 < base + 256:
                                    r0x, _ = XL[nm]
                                    for jn in range(3):
                                        K.dma("sp", xb_view(XBo, r0x + jn * 32, (256, 128))[t0 - base:t0 - base + 128, :],
                                              vst[vs_][:, 256 + jn * 128:256 + (jn + 1) * 128],
                                              [B_vst[vs_]], [B_XBo])
                    ws.release("A")
                    ws.release("A")

                def final_step(tok0, ntok):
                    for (o, w) in _blocks(ntok):
                        rstd_for(lambda c: (h_sb[:, c, o:o + w], B_h), 8, w, ones_bf[:], B_ones, 1.0 / D, 4)
                        for c in range(8):
                            og = st["ostg"]; st["ostg"] ^= 1
                            K.op("dve", lambda e: e.scalar_tensor_tensor(
                                out=ostg[og][:, 0:w], in0=h_sb[:, c, o:o + w], scalar=nfin_sb[:, c:c + 1],
                                in1=rstd_sb[:, 0:w], op0=ALU.mult, op1=ALU.mult),
                                [B_h, B_rstd, B_small], [B_ostg[og]])
                            K.dma("sp", outT[c * 128:(c + 1) * 128, tok0 + o:tok0 + o + w], ostg[og][:, 0:w],
                                  [B_ostg[og]], [B_out])

                for (tok0, ntok, v) in tiles:
                    K.dma("sp", h_sb[:, :, 0:ntok],
                          hT_in.rearrange("(c p) n -> p c n", p=128)[:, :, tok0:tok0 + ntok], [B_hTi], [B_h])
                    if l_prev is not None:
                        wout_step(l_prev, v, tok0, ntok)
                        ffn(l_prev, 2, v, ntok)
                    if l_next is not None:
                        ffn(l_next, 0, v, ntok)
                        proj_step(l_next, v, tok0, ntok)
                    if final:
                        final_step(tok0, ntok)
                    else:
                        K.dma("sp", hT_o.rearrange("(c p) n -> p c n", p=128)[:, :, tok0:tok0 + ntok],
                              h_sb[:, :, 0:ntok], [B_h], [B_hTo])

        ti = 0
        for pi, ph in enumerate(phases):
            K.barrier()
            if ph[0] == "T":
                hin, bhin, hout, bhout = h_io[ti]
                ti += 1
                token_phase(pi, ph[1], ph[2], ph[3], hin, bhin, hout, bhout)
                if ph[2] is not None and S[ph[2]].get("XB") is not None:
                    pack_xb(ph[2])
            elif ph[0] == "X":
                exchange(ph[1])
            else:
                attention_phase(ph[1])
        K.finish()
        P.ninstr = K.ninstr
    return P


def _fm(vec):
    return np.ascontiguousarray(np.asarray(vec, np.float32).reshape(8, 128).T)


def _swap_cols(w):
    n = w.shape[1]
    idx = np.arange(n).reshape(n // 64, 2, 32)[:, ::-1, :].reshape(-1)
    return w[:, idx]


def _prep_layer(inp, l):
    w_in = np.asarray(inp["w_in"][l], np.float32)
    cols = {
        "q_w": (0, 384), "q_g": (384, 640), "q_n": (640, 1024),
        "k_w": (1024, 1152), "v_w": (1152, 1280), "k_g": (1280, 1408),
        "v_g": (1408, 1536), "k_n": (1536, 1920), "v_n": (1920, 2304),
    }

    def chunk(name, i):
        a, _ = cols[name]
        return w_in[:, a + i * 128:a + (i + 1) * 128]

    parts = []
    for nm, i in [("q_w", 0), ("q_w", 1), ("q_w", 2), ("q_g", 0), ("q_g", 1), ("k_w", 0), ("k_g", 0)]:
        c = chunk(nm, i)
        parts += [c, _swap_cols(c)]
    for nm, i in [("q_n", 0), ("q_n", 1), ("q_n", 2), ("k_n", 0), ("k_n", 1), ("k_n", 2)]:
        parts.append(chunk(nm, i))
    wqk = np.ascontiguousarray(np.concatenate(parts, axis=1))
    wv = np.ascontiguousarray(np.concatenate(
        [w_in[:, slice(*cols["v_w"])], w_in[:, slice(*cols["v_g"])], w_in[:, slice(*cols["v_n"])]], axis=1))
    p64 = np.arange(128) % 64
    qg = np.asarray(inp["q_norm_glob"][l], np.float32)
    kg = np.asarray(inp["k_norm_glob"][l], np.float32)
    qkn = np.stack([qg[p64], qg[(p64 + 32) % 64], kg[p64], kg[(p64 + 32) % 64]], axis=1).astype(np.float32)
    nrm = np.stack([_fm(inp["norm_ffn1"][l]), _fm(inp["norm_mix"][l]), _fm(inp["norm_ffn2"][l])], axis=1)
    bT = np.ascontiguousarray(np.asarray(inp["b_ada"][l], np.float32).reshape(72, 128).T)
    rpb = np.asarray(inp["rpb_nbr"][l], np.float32)
    kr = np.arange(2)[:, None, None, None]
    kc = np.arange(64)[None, :, None, None]
    ii = np.arange(22)[None, None, :, None]
    qc = np.arange(64)[None, None, None, :]
    a = 17 - ii + kr + 0 * kc + 0 * qc
    dc = kc - qc + 15 + 0 * ii + 0 * kr
    ok = (a >= 0) & (a <= 14) & (dc >= 0) & (dc <= 30)
    ac = np.clip(a, 0, 14)
    dcc = np.clip(dc, 0, 30)
    zraw = np.where(ok[None], rpb[:, ac, dcc], 0.0).astype(np.float32).reshape(6, 128, 22 * 64)
    return dict(
        wada=np.ascontiguousarray(np.asarray(inp["w_ada"][l], np.float32)), bT=bT, nrm=np.ascontiguousarray(nrm),
        wg1=np.ascontiguousarray(inp["w_ffn1_gate"][l]), wu1=np.ascontiguousarray(inp["w_ffn1_up"][l]),
        wd1=np.ascontiguousarray(inp["w_ffn1_down"][l]),
        wg2=np.ascontiguousarray(inp["w_ffn2_gate"][l]), wu2=np.ascontiguousarray(inp["w_ffn2_up"][l]),
        wd2=np.ascontiguousarray(inp["w_ffn2_down"][l]),
        wout=np.ascontiguousarray(inp["w_out"][l]), wqk=wqk, wv=wv, qkn=np.ascontiguousarray(qkn),
        sinkT=np.ascontiguousarray(np.broadcast_to(np.asarray(inp["sink_win"][l], np.float32)[None, :], (128, 6))),
        zraw=np.ascontiguousarray(zraw),
    )


def _const_tables(TL, half):
    RT = TL // GRID_W
    NQT = TL // 512
    rows_total = 2 * RT
    t = half * TL + np.arange(TL)
    row = (t // GRID_W).astype(np.float32)
    col = (t % GRID_W).astype(np.float32)
    inv = (np.float32(10000.0) ** (-np.arange(16, dtype=np.float32) / np.float32(16))).astype(np.float32)
    ang = np.concatenate([row[:, None] * inv, col[:, None] * inv], axis=-1).astype(np.float32)
    p64 = np.arange(128) % 64
    cosT = np.cos(ang)[:, p64 % 32].T.astype(np.float32)
    sgn = np.where(p64 < 32, -1.0, 1.0).astype(np.float32)
    sinT = (np.sin(ang)[:, p64 % 32].T * sgn[:, None]).astype(np.float32)
    a = np.arange(128)[:, None, None]
    i = np.arange(4)[None, :, None]
    b = np.arange(128)[None, None, :]
    Mw = np.zeros((128, 8, 512), np.float32)
    for m in range(6):
        j = m - 1
        dji = j - i
        ok = (dji == 0) | ((dji == 1) & (a <= b)) | ((dji == -1) & (a >= b))
        Mw[:, m, :] = np.broadcast_to(ok, (128, 4, 128)).reshape(128, 512)
    Mw[:, 6, :] = Mw[:, 0, :] * (1.0 if half == 1 else 0.0)
    Mw[:, 7, :] = Mw[:, 5, :] * (1.0 if half == 0 else 0.0)
    kr = np.arange(2)[:, None, None, None]
    kc = np.arange(64)[None, :, None, None]
    ii = np.arange(22)[None, None, :, None]
    qc = np.arange(64)[None, None, None, :]
    aa = 17 - ii + kr + 0 * kc + 0 * qc
    dc = kc - qc + 15 + 0 * ii + 0 * kr
    cs = np.clip(qc - 8, 0, GRID_W - 16)
    colok = (kc >= cs) & (kc < cs + 16) & (dc >= 0) & (dc <= 30) + 0 * ii + 0 * kr
    colok = np.broadcast_to(colok, (2, 64, 22, 64))
    zfull = ((aa >= 0) & (aa <= 14) & colok).reshape(128, 22 * 64)
    zint = ((aa >= 3) & (aa <= 10) & colok).reshape(128, 22 * 64)
    zmask = np.stack([zfull, zint], axis=1).astype(np.float32)
    mrow = np.zeros((128, 16, 512), np.float32)
    krr = np.arange(2)[:, None, None, None]
    qr = np.arange(8)[None, None, :, None]
    for e, tq in enumerate([0, NQT - 1]):
        for j in range(8):
            Rg = half * RT + 8 * tq - 4 + 2 * j + krr
            rg = half * RT + 8 * tq + qr
            rs = np.clip(rg - 4, 0, rows_total - 8)
            ok = (Rg >= rs) & (Rg < rs + 8)
            mrow[:, e * 8 + j, :] = np.broadcast_to(ok, (2, 64, 8, 64)).reshape(128, 512)
    return dict(cosT=np.ascontiguousarray(cosT), sinT=np.ascontiguousarray(sinT), Mw=Mw.astype(NPBF),
                zmask=np.ascontiguousarray(zmask), mrow=mrow.astype(NPBF))


_PROGS = {}


def _prog(mode, TL, groups):
    key = (mode, TL, str(groups))
    if key not in _PROGS:
        _PROGS[key] = build(mode, TL, groups)
    return _PROGS[key]


def kernel(TL=4096, nbatch=4, **inp):
    inp = {k: np.asarray(v) for k, v in inp.items()}
    x, c, ctx, c_ctx = inp["x"], inp["c"], inp["ctx"], inp["c_ctx"]
    ncore = 2 * nbatch
    groups = [[2 * b, 2 * b + 1] for b in range(nbatch)]
    L = [_prep_layer(inp, l) for l in range(2)]
    tabs = [_const_tables(TL, h) for h in range(2)]
    nfin = _fm(inp["norm_final"])
    maps = []
    for cid in range(ncore):
        b, half = cid // 2, cid % 2
        xT = np.concatenate([x[b, half * TL:(half + 1) * TL, :].T, ctx[b].T], axis=1).astype(np.float32)
        m = dict(hT_i=np.ascontiguousarray(xT),
                 cc=np.ascontiguousarray(np.stack([_fm(c[b]), _fm(c_ctx)], axis=-1)),
                 cosT=tabs[half]["cosT"], sinT=tabs[half]["sinT"], nfin=nfin,
                 Mw=tabs[half]["Mw"], zmask=tabs[half]["zmask"], mrow=tabs[half]["mrow"])
        for l in range(2):
            Ll = L[l]
            m.update({"wada%d" % l: Ll["wada"], "bT%d" % l: Ll["bT"], "nrm%d" % l: Ll["nrm"],
                      "wout%d" % l: Ll["wout"], "wg2_%d" % l: Ll["wg2"], "wu2_%d" % l: Ll["wu2"],
                      "wd2_%d" % l: Ll["wd2"], "wg1_%d" % l: Ll["wg1"], "wu1_%d" % l: Ll["wu1"],
                      "wd1_%d" % l: Ll["wd1"], "wqk%d" % l: Ll["wqk"], "wv%d" % l: Ll["wv"],
                      "qkn%d" % l: Ll["qkn"], "sinkT%d" % l: Ll["sinkT"], "zraw%d" % l: Ll["zraw"]})
        maps.append(m)
    P = _prog("FUSED", TL, groups)
    res = run_bass_kernel_spmd(P.nc, maps, core_ids=list(range(ncore)))
    out = np.zeros((nbatch, 2 * TL, D), np.float32)
    for cid in range(ncore):
        b, half = cid // 2, cid % 2
        out[b, half * TL:(half + 1) * TL, :] = np.asarray(res.results[cid]["outT"], np.float32).T
    return out
```
